# Optimizing a Trainium2 kernel written in Bass

```python
import math
import jax
import jax.numpy as jnp
from jax import lax
import numpy as np

D_MODEL = 1024
BATCH = 32
SEQ = 256
DEPTH = 1
DEC_BATCH = 8
DEC_SEQ = 4096
PAST_LEN = 512

GRID_W = 64
HEAD_DIM = 64
A_Q_HEADS = 8
A_KV_HEADS = 2
A_GROUP = A_Q_HEADS // A_KV_HEADS
B_HEADS = 4
B_V_DIM = 2 * HEAD_DIM
A_WIDTH = A_Q_HEADS * HEAD_DIM
B_WIDTH = B_HEADS * B_V_DIM
A_Q_COLS = A_Q_HEADS * HEAD_DIM
A_KV_COLS = A_KV_HEADS * HEAD_DIM
B_QK_COLS = B_HEADS * 2 * HEAD_DIM
B_V_COLS = B_HEADS * B_V_DIM
SPLIT_POINTS = (A_Q_COLS, A_Q_COLS + A_KV_COLS, A_Q_COLS + 2 * A_KV_COLS,
                A_Q_COLS + 2 * A_KV_COLS + B_QK_COLS, A_Q_COLS + 2 * A_KV_COLS + 2 * B_QK_COLS)
IN_COLS = A_Q_COLS + 2 * A_KV_COLS + 2 * B_QK_COLS + B_V_COLS
N_BRANCHES = 2
D_FF = 4 * D_MODEL
N_MOD = 6
Q_BLOCK = 128
ROPE_BASE = 10000.0
EPS = 1e-6

kernel_name = "hybrid_dit_gqa_diffattn_step"


def rms_norm(x, g):
    xf = x.astype(jnp.float32)
    y = xf * lax.rsqrt(jnp.mean(xf * xf, axis=-1, keepdims=True) + EPS)
    return (y * g.astype(jnp.float32)).astype(x.dtype)


def modulation(cond, w_mod, b_mod):
    m = jax.nn.silu(cond) @ w_mod + b_mod
    return jnp.split(m, N_MOD, axis=-1)


def axial_rope_tables(n_tokens, dtype):
    rows = n_tokens // GRID_W
    row = jnp.broadcast_to(jnp.arange(rows, dtype=jnp.float32)[:, None], (rows, GRID_W)).reshape(-1)
    col = jnp.broadcast_to(jnp.arange(GRID_W, dtype=jnp.float32)[None, :], (rows, GRID_W)).reshape(-1)
    n_freq = HEAD_DIM // 4
    freqs = ROPE_BASE ** (-jnp.arange(n_freq, dtype=jnp.float32) / n_freq)
    ang = jnp.concatenate([row[:, None] * freqs, col[:, None] * freqs], axis=-1)
    return jnp.cos(ang).astype(dtype), jnp.sin(ang).astype(dtype)


def apply_rope(x, cos, sin):
    half = HEAD_DIM // 2
    shape = (1, cos.shape[0]) + (1,) * (x.ndim - 3) + (half,)
    c = cos.reshape(shape)
    s = sin.reshape(shape)
    x1, x2 = x[..., :half], x[..., half:]
    return jnp.concatenate([x1 * c - x2 * s, x1 * s + x2 * c], axis=-1)


def sweep_query_blocks(fn, q):
    b, s = q.shape[:2]
    nb = s // Q_BLOCK
    blocks = jnp.moveaxis(q.reshape((b, nb, Q_BLOCK) + q.shape[2:]), 1, 0)
    out = lax.map(fn, blocks)
    out = jnp.moveaxis(out, 0, 1)
    return out.reshape((b, s) + out.shape[3:])


def gqa_attend(q, k, v):
    scale = HEAD_DIM ** -0.5

    def block(qb):
        qg = qb.reshape(qb.shape[:2] + (A_KV_HEADS, A_GROUP, HEAD_DIM))
        s = jnp.einsum('bqgrd,bkgd->bgrqk', qg, k, preferred_element_type=jnp.float32) * scale
        p = jax.nn.softmax(s, axis=-1).astype(v.dtype)
        o = jnp.einsum('bgrqk,bkgd->bqgrd', p, v)
        return o.reshape(qb.shape)

    return sweep_query_blocks(block, q)


def diff_attend(q, k, v, lam):
    scale = HEAD_DIM ** -0.5

    def block(qb):
        s = jnp.einsum('bqhcd,bkhcd->bhcqk', qb, k, preferred_element_type=jnp.float32) * scale
        p = jax.nn.softmax(s, axis=-1)
        w = (p[:, :, 0] - lam * p[:, :, 1]).astype(v.dtype)
        return jnp.einsum('bhqk,bkhe->bqhe', w, v)

    return sweep_query_blocks(block, q)


def trunk_layer(x, mods, lp, lam_init, rope=None, ctx_kv=None):
    shift1, scale1, gate1, shift2, scale2, gate2 = mods
    b, s, _ = x.shape
    h = rms_norm(x, lp['norm1_g']) * (1 + scale1) + shift1
    a_q, a_k, a_v, b_q, b_k, b_v = jnp.split(h @ lp['w_in'], SPLIT_POINTS, axis=-1)
    a_q = rms_norm(a_q.reshape(b, s, A_Q_HEADS, HEAD_DIM), lp['a_q_norm_g'])
    a_k = rms_norm(a_k.reshape(b, s, A_KV_HEADS, HEAD_DIM), lp['a_k_norm_g'])
    a_v = a_v.reshape(b, s, A_KV_HEADS, HEAD_DIM)
    b_q = b_q.reshape(b, s, B_HEADS, 2, HEAD_DIM)
    b_k = b_k.reshape(b, s, B_HEADS, 2, HEAD_DIM)
    b_v = b_v.reshape(b, s, B_HEADS, B_V_DIM)
    if rope is not None:
        cos, sin = rope
        a_q = apply_rope(a_q, cos, sin)
        a_k = apply_rope(a_k, cos, sin)
        b_q = apply_rope(b_q, cos, sin)
        b_k = apply_rope(b_k, cos, sin)
    own_kv = (a_k, a_v, b_k, b_v)
    if ctx_kv is None:
        keys = own_kv
    else:
        keys = tuple(jnp.concatenate([kc, ko], axis=1) for kc, ko in zip(ctx_kv, own_kv))
    f32 = jnp.float32
    lam = (jnp.exp(jnp.sum(lp['lam_q1'].astype(f32) * lp['lam_k1'].astype(f32)))
           - jnp.exp(jnp.sum(lp['lam_q2'].astype(f32) * lp['lam_k2'].astype(f32))) + lam_init)
    a_o = gqa_attend(a_q, keys[0], keys[1]).reshape(b, s, A_WIDTH)
    b_o = diff_attend(b_q, keys[2], keys[3], lam)
    b_o = (rms_norm(b_o, lp['b_subln_g']) * (1 - lam_init)).reshape(b, s, B_WIDTH)
    gate_a, gate_b = jnp.split(jax.nn.sigmoid(h @ lp['w_gate'] + lp['b_gate']), N_BRANCHES, axis=-1)
    merged = gate_a * (a_o @ lp['w_br_a']) + gate_b * (b_o @ lp['w_br_b'])
    x = x + gate1 * (merged @ lp['w_out'])
    h2 = rms_norm(x, lp['norm2_g']) * (1 + scale2) + shift2
    x = x + gate2 * (jnp.square(jax.nn.relu(h2 @ lp['w_fc1'])) @ lp['w_fc2'])
    return x, own_kv


def setup_inputs(seed: int = 0) -> dict:
    key = jax.random.key(seed)
    ks = iter(jax.random.split(key, 40))

    def nrm(shape, scale):
        return scale * jax.random.normal(next(ks), shape, jnp.float32)

    def gain(shape):
        return 1.0 + nrm(shape, 0.02)

    d = D_MODEL
    return {
        'x_prompt': nrm((BATCH, SEQ, d), 1.0),
        'x_sample': nrm((DEC_BATCH, DEC_SEQ, d), 1.0),
        'cache_a_k': nrm((DEC_BATCH, DEPTH, PAST_LEN, A_KV_HEADS, HEAD_DIM), 1.0),
        'cache_a_v': nrm((DEC_BATCH, DEPTH, PAST_LEN, A_KV_HEADS, HEAD_DIM), 1.0),
        'cache_b_k': nrm((DEC_BATCH, DEPTH, PAST_LEN, B_HEADS, 2, HEAD_DIM), 1.0),
        'cache_b_v': nrm((DEC_BATCH, DEPTH, PAST_LEN, B_HEADS, B_V_DIM), 1.0),
        'c': nrm((DEC_BATCH, d), 1.0),
        'c_ctx': nrm((d,), 1.0),
        'w_mod': nrm((DEPTH, d, N_MOD * d), d ** -0.5),
        'b_mod': nrm((DEPTH, N_MOD * d), 0.02),
        'norm1_g': gain((DEPTH, d)),
        'w_in': nrm((DEPTH, d, IN_COLS), d ** -0.5),
        'a_q_norm_g': gain((DEPTH, HEAD_DIM)),
        'a_k_norm_g': gain((DEPTH, HEAD_DIM)),
        'lam_q1': nrm((DEPTH, HEAD_DIM), 0.1),
        'lam_k1': nrm((DEPTH, HEAD_DIM), 0.1),
        'lam_q2': nrm((DEPTH, HEAD_DIM), 0.1),
        'lam_k2': nrm((DEPTH, HEAD_DIM), 0.1),
        'b_subln_g': gain((DEPTH, B_V_DIM)),
        'w_gate': nrm((DEPTH, d, N_BRANCHES * d), d ** -0.5),
        'b_gate': nrm((DEPTH, N_BRANCHES * d), 0.02),
        'w_br_a': nrm((DEPTH, A_WIDTH, d), A_WIDTH ** -0.5),
        'w_br_b': nrm((DEPTH, B_WIDTH, d), B_WIDTH ** -0.5),
        'w_out': nrm((DEPTH, d, d), d ** -0.5),
        'norm2_g': gain((DEPTH, d)),
        'w_fc1': nrm((DEPTH, d, D_FF), d ** -0.5),
        'w_fc2': nrm((DEPTH, D_FF, d), D_FF ** -0.5),
        'final_norm_g': gain((d,)),
    }


def reference(x_prompt, x_sample, cache_a_k, cache_a_v, cache_b_k, cache_b_v, c, c_ctx,
              w_mod, b_mod, norm1_g, w_in, a_q_norm_g, a_k_norm_g, lam_q1, lam_k1, lam_q2, lam_k2,
              b_subln_g, w_gate, b_gate, w_br_a, w_br_b, w_out, norm2_g, w_fc1, w_fc2, final_norm_g):
    rope = axial_rope_tables(x_sample.shape[1], x_sample.dtype)
    xp, xs = x_prompt, x_sample
    new_a_k, new_a_v, new_b_k, new_b_v = [], [], [], []
    for l in range(DEPTH):
        lp = {
            'norm1_g': norm1_g[l], 'w_in': w_in[l], 'a_q_norm_g': a_q_norm_g[l], 'a_k_norm_g': a_k_norm_g[l],
            'lam_q1': lam_q1[l], 'lam_k1': lam_k1[l], 'lam_q2': lam_q2[l], 'lam_k2': lam_k2[l],
            'b_subln_g': b_subln_g[l], 'w_gate': w_gate[l], 'b_gate': b_gate[l],
            'w_br_a': w_br_a[l], 'w_br_b': w_br_b[l], 'w_out': w_out[l],
            'norm2_g': norm2_g[l], 'w_fc1': w_fc1[l], 'w_fc2': w_fc2[l],
        }
        lam_init = 0.8 - 0.6 * math.exp(-0.3 * l)
        mods_ctx = modulation(c_ctx, w_mod[l], b_mod[l])
        mods_lat = [m[:, None, :] for m in modulation(c, w_mod[l], b_mod[l])]
        xp, kv_ctx = trunk_layer(xp, mods_ctx, lp, lam_init)
        new_a_k.append(kv_ctx[0])
        new_a_v.append(kv_ctx[1])
        new_b_k.append(kv_ctx[2])
        new_b_v.append(kv_ctx[3])
        ctx_kv = (cache_a_k[:, l], cache_a_v[:, l], cache_b_k[:, l], cache_b_v[:, l])
        xs, _ = trunk_layer(xs, mods_lat, lp, lam_init, rope=rope, ctx_kv=ctx_kv)
    y_prompt = rms_norm(xp, final_norm_g)
    y_sample = rms_norm(xs, final_norm_g)
    return (y_prompt, y_sample, jnp.stack(new_a_k, axis=1), jnp.stack(new_a_v, axis=1),
            jnp.stack(new_b_k, axis=1), jnp.stack(new_b_v, axis=1))
```

```python
import numpy as np
import concourse.bass as bass
import concourse.mybir as mybir
from concourse.bass_utils import run_bass_kernel_spmd

F32 = mybir.dt.float32
BF16 = mybir.dt.bfloat16
AF = mybir.ActivationFunctionType
ALU = mybir.AluOpType
AX = mybir.AxisListType

D = 1024
HD = 64
EPS = 1e-6
LAM_INIT = 0.2
N_CORES = 8


class _Op:
    __slots__ = ("id", "eng", "fn", "deps", "dma", "sem", "skey", "val", "has_dep")

    def __init__(self, id, eng, fn, dma):
        self.id = id
        self.eng = eng
        self.fn = fn
        self.deps = {}
        self.dma = dma
        self.sem = None
        self.skey = None
        self.val = 0
        self.has_dep = False


class Sched:
    ENGS = ("sp", "act", "pe", "dve", "pool")
    NDMA = 8

    def __init__(self):
        self.ops = []
        self.last_w = {}
        self.readers = {}
        self.eng_ops = {e: [] for e in self.ENGS}
        self.dma_hist = {e: [] for e in self.ENGS}
        self.bar = ()

    def add(self, eng, fn, r=(), w=(), dma=False):
        op = _Op(len(self.ops), eng, fn, dma)
        deps = op.deps
        for b in self.bar:
            deps[b] = True
        for k in r:
            lw = self.last_w.get(k)
            if lw is not None:
                deps[lw] = True
        for k in w:
            lw = self.last_w.get(k)
            if lw is not None:
                deps.setdefault(lw, False)
            for rd in self.readers.get(k, ()):
                deps.setdefault(rd, False)
        for k in r:
            self.readers.setdefault(k, []).append(op.id)
        for k in w:
            self.last_w[k] = op.id
            self.readers[k] = []
        if dma:
            h = self.dma_hist[eng]
            if len(h) >= self.NDMA:
                deps.setdefault(h[-self.NDMA], True)
            h.append(op.id)
        self.ops.append(op)
        self.eng_ops[eng].append(op)
        return op

    def barrier(self):
        bar = []
        for e in self.ENGS:
            lst = self.eng_ops[e]
            last_c = None
            for op in reversed(lst):
                if not op.dma:
                    last_c = op.id
                    break
            if last_c is not None:
                bar.append(last_c)
            bar.extend(self.dma_hist[e][-self.NDMA:])
        self.bar = tuple(bar)
        self.last_w = {}
        self.readers = {}

    @staticmethod
    def _needs_wait(cons, prod, is_raw):
        if prod.dma or prod.eng != cons.eng:
            return True
        if cons.dma:
            return True
        if cons.eng == "pe":
            return False
        return True

    def emit(self, eng_sems, dma_sems):
        ops = self.ops
        for op in ops:
            keep = {}
            for d, raw in op.deps.items():
                p = ops[d]
                if self._needs_wait(op, p, raw):
                    keep[d] = raw
                    p.has_dep = True
            op.deps = keep
        cnt = {e: 0 for e in self.ENGS}
        dcnt = {e: [0] * self.NDMA for e in self.ENGS}
        dn = {e: 0 for e in self.ENGS}
        for op in ops:
            if op.dma:
                i = dn[op.eng] % self.NDMA
                dn[op.eng] += 1
                dcnt[op.eng][i] += 16
                op.sem = dma_sems[op.eng][i]
                op.skey = ("d", op.eng, i)
                op.val = dcnt[op.eng][i]
            elif op.has_dep:
                cnt[op.eng] += 1
                op.sem = eng_sems[op.eng]
                op.skey = ("e", op.eng)
                op.val = cnt[op.eng]
        self.stats = {"waits": 0, "incs": dict(cnt), "ops": len(ops)}
        finals = []
        for e in self.ENGS:
            for i in range(self.NDMA):
                if dcnt[e][i]:
                    finals.append((dma_sems[e][i], dcnt[e][i]))

        def run(engname, e):
            known = {}
            for op in self.eng_ops[engname]:
                need = {}
                for d in op.deps:
                    p = ops[d]
                    key = p.skey
                    if known.get(key, 0) < p.val and (key not in need or need[key][1] < p.val):
                        need[key] = (p.sem, p.val)
                for key, (s, v) in need.items():
                    e.wait_ge(s, v)
                    known[key] = v
                    self.stats["waits"] += 1
                ins = op.fn(e)
                if op.dma:
                    ins.then_inc(op.sem, 16)
                elif op.has_dep:
                    ins.then_inc(op.sem, 1)
            if engname == "sp":
                for s, v in finals:
                    e.wait_ge(s, v)

        return run


class Arena:
    def __init__(self, t, nwords):
        self.t = t
        self.n = nwords
        self.off = 0
        self.peak = 0

    def mark(self):
        return self.off

    def release(self, m):
        self.off = m

    def _take(self, words):
        words = (words + 7) // 8 * 8
        o = self.off
        self.off += words
        assert self.off <= self.n, f"arena overflow {self.off} > {self.n}"
        self.peak = max(self.peak, self.off)
        return o

    def f32(self, *shape, parts=128):
        n = int(np.prod(shape))
        o = self._take(n)
        ap = self.t[0:parts, o:o + n]
        return self._shape(ap, shape)

    def bf16(self, *shape, parts=128):
        n = int(np.prod(shape))
        o = self._take((n + 1) // 2)
        ap = self.t[0:parts, o:o + (n + 1) // 2].bitcast(BF16)[:, 0:n]
        return self._shape(ap, shape)

    @staticmethod
    def _shape(ap, shape):
        if len(shape) == 1:
            return ap
        if len(shape) == 2:
            return ap.rearrange("p (a b) -> p a b", b=shape[1])
        if len(shape) == 3:
            return ap.rearrange("p (a b c) -> p a b c", b=shape[1], c=shape[2])
        raise ValueError(shape)


def build_program(NP=4, SEQ=256, DS=4096, PAST=512, arena_words=53120, phases="MABXC"):
    assert DS % 512 == 0 and SEQ % 128 == 0 and PAST % 128 == 0
    TALL = DS + NP * SEQ
    nc = bass.Bass("TRN2", target_bir_lowering=False)

    def din(name, shape):
        return nc.dram_tensor(name, list(shape), F32, kind="ExternalInput").ap()

    x_all = din("x_all", [TALL, D])
    cak = din("cak", [PAST, 128])
    cav = din("cav", [PAST, 128])
    cbk = din("cbk", [PAST, 512])
    cbv = din("cbv", [PAST, 512])
    cvec = din("cvec", [2, D])
    w_mod = din("w_mod", [D, 6 * D])
    b_mod = din("b_mod", [1, 6 * D])
    norm1_g = din("norm1_g", [1, D])
    w_in = din("w_in", [D, 2304])
    a_q_norm_g = din("a_q_norm_g", [1, HD])
    a_k_norm_g = din("a_k_norm_g", [1, HD])
    lam_q1 = din("lam_q1", [1, HD])
    lam_k1 = din("lam_k1", [1, HD])
    lam_q2 = din("lam_q2", [1, HD])
    lam_k2 = din("lam_k2", [1, HD])
    b_subln_g = din("b_subln_g", [1, 128])
    w_gate = din("w_gate", [D, 2 * D])
    b_gate = din("b_gate", [1, 2 * D])
    w_br_a = din("w_br_a", [512, D])
    w_br_b = din("w_br_b", [512, D])
    w_out = din("w_out", [D, D])
    norm2_g = din("norm2_g", [1, D])
    w_fc1 = din("w_fc1", [D, 4 * D])
    w_fc2 = din("w_fc2", [4 * D, D])
    final_norm_g = din("final_norm_g", [1, D])
    ident_d = din("ident", [128, 128])

    y_all = nc.dram_tensor("y_all", [TALL, D], F32, kind="ExternalOutput").ap()
    nkv = nc.dram_tensor("nkv", [NP * SEQ, 1280], F32, kind="ExternalOutput").ap()

    hT_scr = nc.dram_tensor("hT_scr", [128, 8, TALL], BF16, kind="Internal").ap()
    rope_d = nc.dram_tensor("rope_scr", [DS, 128], F32, kind="Internal").ap()
    qT_scr = nc.dram_tensor("qT_scr", [128, 8, TALL], BF16, kind="Internal").ap()
    ao_scr = nc.dram_tensor("ao_scr", [64, 8, TALL], BF16, kind="Internal").ap()
    bo_scr = nc.dram_tensor("bo_scr", [128, 4, TALL], BF16, kind="Internal").ap()
    x1_scr = nc.dram_tensor("x1_scr", [TALL, D], F32, kind="Internal").ap()
    mods_scr = nc.dram_tensor("mods_scr", [2, 6 * D], F32, kind="Internal").ap()
    wg_bf = nc.dram_tensor("wg_bf", [D, 2 * D], BF16, kind="Internal").ap()
    wba_bf = nc.dram_tensor("wba_bf", [512, D], BF16, kind="Internal").ap()
    wbb_bf = nc.dram_tensor("wbb_bf", [512, D], BF16, kind="Internal").ap()
    wo_bf = nc.dram_tensor("wo_bf", [D, D], BF16, kind="Internal").ap()
    w1_bf = nc.dram_tensor("w1_bf", [D, 4 * D], BF16, kind="Internal").ap()
    w2_bf = nc.dram_tensor("w2_bf", [4 * D, D], BF16, kind="Internal").ap()

    S = Sched()

    def dma(q, out, in_, r=(), w=(), slow=False):
        if slow:
            S.add(q, lambda e: e.dma_start(out=out, in_=in_, allow_slow_non_contiguous=True), r=r, w=w, dma=True)
        else:
            S.add(q, lambda e: e.dma_start(out=out, in_=in_), r=r, w=w, dma=True)

    def mm(out, lhsT, rhs, start, stop, r=(), w=()):
        S.add("pe", lambda e: e.matmul(out, lhsT=lhsT, rhs=rhs, start=start, stop=stop), r=r, w=w)

    def act(out, in_, func, r=(), w=(), scale=1.0, bias=None, accum=None):
        def f(e):
            kw = {}
            if bias is not None:
                kw["bias"] = bias
            if accum is not None:
                kw["accum_out"] = accum
            return e.activation(out=out, in_=in_, func=func, scale=scale, **kw)
        S.add("act", f, r=r, w=w)

    def cp(eng, out, in_, r=(), w=()):
        if eng == "act":
            S.add("act", lambda e: e.copy(out=out, in_=in_), r=r, w=w)
        else:
            S.add(eng, lambda e: e.tensor_copy(out=out, in_=in_), r=r, w=w)

    def tt(eng, out, a, b, op, r=(), w=()):
        S.add(eng, lambda e: e.tensor_tensor(out=out, in0=a, in1=b, op=op), r=r, w=w)

    def stt(eng, out, in0, scalar, in1, op0, op1, r=(), w=()):
        S.add(eng, lambda e: e.scalar_tensor_tensor(out=out, in0=in0, scalar=scalar, in1=in1, op0=op0, op1=op1), r=r, w=w)

    def ts(eng, out, in0, s1, s2, op0, op1=None, r=(), w=()):
        def f(e):
            if op1 is None:
                return e.tensor_scalar(out=out, in0=in0, scalar1=s1, scalar2=None, op0=op0)
            return e.tensor_scalar(out=out, in0=in0, scalar1=s1, scalar2=s2, op0=op0, op1=op1)
        S.add(eng, f, r=r, w=w)

    def recip(out, in_, r=(), w=()):
        S.add("dve", lambda e: e.reciprocal(out=out, in_=in_), r=r, w=w)

    def redsum(out, in_, r=(), w=()):
        S.add("dve", lambda e: e.reduce_sum(out=out, in_=in_, axis=AX.X), r=r, w=w)

    def memset(eng, ap, val, w=()):
        S.add(eng, lambda e: e.memset(ap, val), w=w)

    from contextlib import ExitStack
    with ExitStack() as st:
        arena_t = st.enter_context(nc.sbuf_tensor("arena", [128, arena_words], F32))
        ps = st.enter_context(nc.psum_tensor("ps", [128, 4096], F32))
        eng_sems = {e: st.enter_context(nc.semaphore("s_" + e)) for e in Sched.ENGS}
        dma_sems = {e: [st.enter_context(nc.semaphore(f"d_{e}{i}")) for i in range(Sched.NDMA)]
                    for e in ("sp", "act", "pool")}
        for e in ("pe", "dve"):
            dma_sems[e] = [None] * Sched.NDMA
        AR = Arena(arena_t, arena_words)

        def bank(b, n=1):
            return ps[:, b * 512:(b + n) * 512]

        def bankbf(b):
            return bank(b).bitcast(BF16)

        def PS(*bs):
            return [("ps", b) for b in bs]

        def transpose(out, in_, r=(), w=()):
            S.add("pe", lambda e: e.transpose(out=out, in_=in_, identity=identb), r=list(r) + ["ident"], w=w)

        identb = AR.bf16(128)
        eps_t = AR.f32(1)
        ones_bf = AR.bf16(128)
        ones_row = AR.f32(128)
        ones_f = AR.f32(128)
        sel65 = AR.f32(64)
        neg_lam = AR.f32(1)
        gs08 = AR.f32(1)
        gk_bc = AR.f32(64)
        gq_bc = AR.f32(64)
        bgT = AR.f32(16)
        bc = [AR.f32(D) for _ in range(5)]
        BK = [("bc", i) for i in range(5)]
        m_w_in = AR.mark()
        w_in_sb = AR.bf16(8, 2304)
        w_in_r = w_in.rearrange("(kc p) n -> p kc n", p=128)
        for kc in range(8):
            dma("pool", w_in_sb[:, kc, :], w_in_r[:, kc, :], w=["w_in"])
        m_init = AR.mark()
        identf = AR.f32(128)
        lamtmp = AR.f32(4, 64)
        lamred = AR.f32(4)

        dma("sp", identf, ident_d, w=["identf"])
        cp("dve", identb, identf, r=["identf"], w=["ident"])
        memset("dve", eps_t, EPS, w=["eps"])
        memset("dve", ones_bf, 1.0, w=["ones_bf"])
        memset("dve", ones_row, 1.0, w=["ones_row"])
        memset("dve", ones_f, 1.0, w=["ones_f"])
        memset("dve", sel65, 0.0, w=["sel65"])
        memset("pool", sel65[64:65, :], 1.0, w=["sel65"])
        dma("sp", gk_bc, a_k_norm_g[0, :].partition_broadcast(128), w=["gk"])
        dma("sp", gq_bc, a_q_norm_g[0, :].partition_broadcast(128), w=["gq"])
        def late_init():
            dma("act", bgT, b_gate.rearrange("o (oc p) -> p (o oc)", p=128), w=["bgT"], slow=True)
            dma("act", gs08, b_subln_g.rearrange("o p -> p o"), w=["gs08"], slow=True)
            ts("dve", gs08, gs08, 1.0 - LAM_INIT, None, ALU.mult, r=["gs08"], w=["gs08"])
            for i, v in enumerate((lam_q1, lam_k1, lam_q2, lam_k2)):
                dma("act", lamtmp[:, i, :], v[0, :].partition_broadcast(128), w=[("lam", i)])
            tt("dve", lamtmp[:, 0, :], lamtmp[:, 0, :], lamtmp[:, 1, :], ALU.mult, r=[("lam", 0), ("lam", 1)], w=[("lam", 0)])
            tt("dve", lamtmp[:, 2, :], lamtmp[:, 2, :], lamtmp[:, 3, :], ALU.mult, r=[("lam", 2), ("lam", 3)], w=[("lam", 2)])
            redsum(lamred[:, 0:1], lamtmp[:, 0, :], r=[("lam", 0)], w=["lamred0"])
            redsum(lamred[:, 1:2], lamtmp[:, 2, :], r=[("lam", 2)], w=["lamred1"])
            act(lamred[:, 0:2], lamred[:, 0:2], AF.Exp, r=["lamred0", "lamred1"], w=["lamred0", "lamred1"])
            stt("dve", neg_lam, lamred[:, 1:2], -LAM_INIT, lamred[:, 0:1], ALU.add, ALU.subtract,
                r=["lamred0", "lamred1"], w=["neg_lam"])

            import math
            ntl = DS // 128
            pidx = AR.f32(1)
            r64 = AR.f32(1)
            colv = AR.f32(1)
            negpi = AR.f32(1)
            pospi = AR.f32(1)
            jidx = AR.f32(16)
            freqs = AR.f32(16)
            angc = AR.f32(16)
            rowv = AR.f32(ntl)
            ang = AR.f32(ntl, 32)
            msin = AR.f32(ntl, 32)
            mcos = AR.f32(ntl, 32)
            Ttab = AR.f32(ntl, 128)
            S.add("pool", lambda e: e.iota(pidx, [[0, 1]], base=0, channel_multiplier=1, allow_small_or_imprecise_dtypes=True), w=["pidx"])
            S.add("pool", lambda e: e.iota(jidx, [[1, 16]], base=0, channel_multiplier=0, allow_small_or_imprecise_dtypes=True), w=["jidx"])
            S.add("pool", lambda e: e.iota(rowv, [[2, ntl]], base=0, channel_multiplier=0, allow_small_or_imprecise_dtypes=True), w=["rowv"])
            memset("dve", negpi, -math.pi, w=["negpi"])
            memset("dve", pospi, math.pi, w=["pospi"])
            S.add("dve", lambda e: e.tensor_single_scalar(out=r64, in_=pidx, scalar=64.0, op=ALU.is_ge), r=["pidx"], w=["r64"])
            stt("dve", colv, r64, -64.0, pidx, ALU.mult, ALU.add, r=["r64", "pidx"], w=["colv"])
            act(freqs, jidx, AF.Exp, r=["jidx"], w=["freqs"], scale=-math.log(10000.0) / 16.0)
            ts("dve", rowv, rowv, r64, None, ALU.add, r=["rowv", "r64"], w=["rowv"])
            tt("dve", ang[:, :, 0:16], rowv.unsqueeze(2).to_broadcast([128, ntl, 16]), freqs.unsqueeze(1).to_broadcast([128, ntl, 16]),
               ALU.mult, r=["rowv", "freqs"], w=["ang_r"])
            ts("dve", angc, freqs, colv, None, ALU.mult, r=["freqs", "colv"], w=["angc"])
            cp("dve", ang[:, :, 16:32], angc.unsqueeze(1).to_broadcast([128, ntl, 16]), r=["angc"], w=["ang_c"])
            kacc = AR.f32(ntl, 32)
            maxang = float(max(DS // 64 - 1, 63)) + 1.5 * math.pi
            nthr = int(maxang / (2.0 * math.pi)) + 1

            def reduce_angle(dst, kdst, shift):
                ts("dve", dst, ang, shift, None, ALU.add, r=["ang_r", "ang_c"], w=[kdst])
                memset("dve", kacc, 0.0, w=["kacc"])
                for i in range(1, nthr + 1):
                    stt("dve", kacc, dst, (2 * i - 1) * math.pi, kacc, ALU.is_ge, ALU.add, r=[kdst, "kacc"], w=["kacc"])
                stt("dve", dst, kacc, -2.0 * math.pi, dst, ALU.mult, ALU.add, r=["kacc", kdst], w=[kdst])

            reduce_angle(msin, "msin", 0.0)
            reduce_angle(mcos, "mcos", 0.5 * math.pi)
            act(Ttab[:, :, 0:32], mcos, AF.Sin, r=["mcos"], w=["T0"])
            act(Ttab[:, :, 96:128], msin, AF.Sin, r=["msin"], w=["T3"])
            act(Ttab[:, :, 64:96], msin, AF.Sin, r=["msin"], w=["T2"], scale=-1.0)
            cp("dve", Ttab[:, :, 32:64], Ttab[:, :, 0:32], r=["T0"], w=["T1"])
            dma("act", rope_d.rearrange("(t p) c -> p t c", p=128), Ttab, r=["T0", "T1", "T2", "T3"], w=["rope_scr"])

        def load_bc(i, row, j):
            dma("sp", bc[i], mods_scr[row, j * D:(j + 1) * D].partition_broadcast(128), r=["mods_scr"], w=[BK[i]])

        def load_gm(i, row, jscale, g_dram):
            load_bc(4, row, jscale)
            dma("sp", bc[i], g_dram[0, :].partition_broadcast(128), w=[BK[i]])
            stt("dve", bc[i], bc[4], 1.0, bc[i], ALU.add, ALU.mult, r=[BK[4], BK[i]], w=[BK[i]])

        if "M" not in phases:
            late_init()
            S.barrier()
            AR.release(m_init)
        if "M" in phases:
            m0 = m_init
            cT = AR.f32(2, 8)
            sg = AR.f32(2, 8)
            s2 = AR.f32(8, 2)
            wblk = [AR.f32(8, 512) for _ in range(2)]
            mods_sb = AR.f32(6 * D, parts=2)
            bmod_sb = AR.f32(6 * D, parts=2)
            dma("sp", cT, cvec.rearrange("r (p j) -> p r j", j=8), w=["cT"])
            dma("sp", bmod_sb, b_mod[0, :].partition_broadcast(2), w=["bmod"])
            act(sg, cT, AF.Sigmoid, r=["cT"], w=["sg"])
            tt("dve", cT, cT, sg, ALU.mult, r=["cT", "sg"], w=["cT"])
            cp("dve", s2, cT.rearrange("p r j -> p j r"), r=["cT"], w=["s2"])
            wm = w_mod.rearrange("(p j) n -> p j n", j=8)
            for nb in range(2):
                dma("sp", wblk[nb], wm[:, :, nb * 512:(nb + 1) * 512], w=[("wblk", nb)])
            for nb in range(12):
                if nb == 1:
                    late_init()
                wb = wblk[nb % 2]
                if nb >= 2:
                    dma("sp", wb, wm[:, :, nb * 512:(nb + 1) * 512], w=[("wblk", nb % 2)])
                pb = nb % 8
                for j in range(8):
                    mm(bank(pb)[0:2, :], s2[:, j, :], wb[:, j, :], j == 0, j == 7,
                       r=["s2", ("wblk", nb % 2)], w=PS(pb))
                tt("dve", mods_sb[:, nb * 512:(nb + 1) * 512], bank(pb)[0:2, :], bmod_sb[:, nb * 512:(nb + 1) * 512],
                   ALU.add, r=["bmod"], w=PS(pb) + ["mods_sb"])
            dma("sp", mods_scr, mods_sb, r=["mods_sb"], w=["mods_scr"])
            S.barrier()
            AR.release(m0)

        seqs = [dict(kind="s", tok0=0, ntok=DS, nctx=PAST, row=0),
                dict(kind="p", tok0=DS, ntok=NP * SEQ, nctx=0, row=1)]

        def rstd_small(ss, n, scale, parts=128):
            def go(rk, wk):
                act(ss, ss, AF.Ln, r=rk + ["eps"], w=wk, scale=scale, bias=eps_t[0:parts, :])
                act(ss, ss, AF.Exp, r=wk, w=wk, scale=-0.5)
            return go

        def norm_to_bf16(xt, kx, junkA, hg, khg, ssb, kss, hb, khb):
            act(junkA, xt, AF.Square, r=list(kx), w=[khb if junkA is hb else "junkA", kss], accum=ssb)
            rstd_small(ssb, 1, 1.0 / D)([kss], [kss])
            stt("dve", hg, xt, ssb, bc[0], ALU.mult, ALU.mult, r=list(kx) + [kss, BK[0]], w=list(khg))
            tt("dve", hb, hg, bc[1], ALU.add, r=list(khg) + [BK[1]], w=[khb])

        if "A" in phases or "B" in phases:
            mAB = AR.mark()
            TKmax = PAST + DS
            NKTmax = TKmax // 128
            KT_A = AR.bf16(TKmax)
            V_A = AR.bf16(NKTmax, 2 * 65)
            KT_B = AR.bf16(4, TKmax)
            V_B = AR.bf16(NKTmax, 512)
            V_A4 = V_A.rearrange("p k (g e) -> p k g e", e=65)
            memset("dve", V_A4[:, :, :, 64:65], 1.0, w=["VA_ones"])
            cur_row = [None]

            for sq in seqs:
                kind, tok0, ntok, nctx, row = sq["kind"], sq["tok0"], sq["ntok"], sq["nctx"], sq["row"]
                rope_on = kind == "s"
                TK = nctx + ntok
                NKT = TK // 128
                ntile = ntok // 128
                mA = AR.mark()
                if cur_row[0] != row:
                    load_gm(0, row, 1, norm1_g)
                    load_bc(1, row, 0)
                    cur_row[0] = row
                xt = [AR.f32(D) for _ in range(2)]
                junkA = AR.bf16(D)
                hg = AR.f32(D)
                hb = AR.bf16(D)
                hTt = [AR.bf16(8, 128) for _ in range(2)]
                QTt = [AR.bf16(8, 128) for _ in range(2)]
                kvf = AR.f32(1280)
                qf = AR.f32(512)
                sqt = AR.f32(640)
                sm = AR.f32(16)
                ropet = [AR.f32(128) for _ in range(2)]
                rt = AR.f32(16, 64)
                ru = AR.f32(16, 64)
                kb = AR.bf16(640)
                qb = AR.bf16(D)
                if nctx:
                    cbk_sb = AR.bf16(nctx // 128, 512)
                    cak_sb = AR.bf16(nctx // 128, 128)
                    nct = nctx // 128
                    dma("pool", cak_sb, cak.rearrange("(t p) c -> p t c", p=128), w=["cak_sb"])
                    dma("pool", cbk_sb, cbk.rearrange("(t p) c -> p t c", p=128), w=["cbk_sb"])
                    for t in range(nct):
                        dma("pool", V_A4[:, t, :, 0:64],
                            cav[t * 128:(t + 1) * 128, :].rearrange("p (g e) -> p g e", e=64), r=["VA_ones"], w=[("VA", t)])
                    dma("pool", V_B[:, 0:nct, :], cbv.rearrange("(t p) c -> p t c", p=128), w=["VB_ctx"])
                    for t in range(nct):
                        bk = 6 + (t % 2)
                        pT = bankbf(bk)
                        transpose(pT[:, 0:128], cak_sb[:, t, :], r=["cak_sb"], w=PS(bk))
                        for h in range(4):
                            transpose(pT[:, 128 * (h + 1):128 * (h + 2)], cbk_sb[:, t, h * 128:(h + 1) * 128],
                                      r=["cbk_sb"], w=PS(bk))
                        cp("act", KT_A[:, t * 128:(t + 1) * 128], pT[:, 0:128], w=PS(bk) + [("KTA", t)])
                        cp("act", KT_B[:, :, t * 128:(t + 1) * 128],
                           pT[:, 128:640].rearrange("p (h k) -> p h k", k=128), w=PS(bk) + [("KTB", t)])

                def loadx(t):
                    dma("sp", xt[t % 2], x_all[tok0 + t * 128: tok0 + (t + 1) * 128, :], w=[("xt", t % 2)])

                def loadrope(t):
                    if rope_on:
                        dma("sp", ropet[t % 2], rope_d[t * 128:(t + 1) * 128, :], w=[("rope", t % 2)])

                bT, bX, bY, bZ = 0, 1, 2, 3
                bQ = ((4, 5), (6, 7))
                bTK = bTQ = bT
                def normAD(t):
                    s = t % 2
                    norm_to_bf16(xt[s], [("xt", s)], junkA, hg, ["hg"], sm[:, 0:1], "ssA", hb, "hb")

                def tr_h(t):
                    s = t % 2
                    pT = bankbf(bT)
                    for kc in range(8):
                        transpose(pT[:, kc * 128:(kc + 1) * 128], hb[:, kc * 128:(kc + 1) * 128], r=["hb"], w=PS(bT))
                    cp("act", hTt[s], pT.rearrange("p (a b) -> p a b", b=128), w=PS(bT) + [("hTt", s)])
                    dma("sp", hT_scr[:, :, tok0 + t * 128: tok0 + (t + 1) * 128], hTt[s], r=[("hTt", s)], w=["hT_scr"])

                def proj(t, specs):
                    s = t % 2
                    for (bb, c0, n) in specs:
                        for kc in range(8):
                            mm(bank(bb)[:, 0:n], hTt[s][:, kc, :], w_in_sb[:, kc, c0:c0 + n], kc == 0, kc == 7,
                               r=[("hTt", s), "w_in"], w=PS(bb))

                def projQ(t):
                    proj(t, ((bQ[t % 2][0], 0, 512), (bQ[t % 2][1], 768, 512)))

                def projK(t):
                    proj(t, ((bX, 512, 256), (bY, 1280, 512), (bZ, 1792, 512)))

                def rope_blk(s, hf, src, nb, rk, wk_src, dst, dkey, perm=False):
                    rp = ropet[s]
                    C2 = rp[:, 0:64].unsqueeze(1)
                    Sn = rp[:, 64:96].unsqueeze(1)
                    Sp = rp[:, 96:128].unsqueeze(1)
                    kr = ("rope", s)
                    t_ = rt[:, hf * 8:hf * 8 + nb, :]
                    u_ = ru[:, hf * 8:hf * 8 + nb, :]
                    kt_, ku1, ku2 = ("rt", hf), ("ru1", hf), ("ru2", hf)
                    tt("dve", t_, src, C2.to_broadcast([128, nb, 64]), ALU.mult, r=rk + [kr], w=wk_src + [kt_])
                    tt("dve", u_[:, :, 0:32], src[:, :, 32:64], Sn.to_broadcast([128, nb, 32]), ALU.mult,
                       r=rk + [kr], w=wk_src + [ku1])
                    tt("dve", u_[:, :, 32:64], src[:, :, 0:32], Sp.to_broadcast([128, nb, 32]), ALU.mult,
                       r=rk + [kr], w=wk_src + [ku2])
                    if perm:
                        tt("dve", dst, t_.rearrange("p (g r) d -> p g r d", g=2),
                           u_.rearrange("p (g r) d -> p g r d", g=2), ALU.add, r=[kt_, ku1, ku2], w=[dkey])
                    else:
                        tt("dve", dst, t_, u_, ALU.add, r=[kt_, ku1, ku2], w=[dkey])

                qbA = qb[:, 0:512].rearrange("p (r g d) -> p g r d", g=2, d=64)
                qbB = qb[:, 512:1024].rearrange("p (b d) -> p b d", d=64)
                kb3 = kb.rearrange("p (b d) -> p b d", d=64)

                def postQ(t):
                    s = t % 2
                    bQA, bQB = bQ[t % 2]
                    pQA = bank(bQA)
                    pQBv = bank(bQB).rearrange("p (b d) -> p b d", d=64)
                    act(sqt[:, 0:512], pQA, AF.Square, w=PS(bQA) + ["sqq"])
                    redsum(sm[:, 2:10], sqt[:, 0:512].rearrange("p (h d) -> p h d", d=64), r=["sqq"], w=["ssq"])
                    rstd_small(sm[:, 2:10], 8, 1.0 / HD)(["ssq"], ["ssq"])
                    if rope_on:
                        rope_blk(s, 1, pQBv, 8, [], PS(bQB), qbB, "qbB")
                    else:
                        cp("dve", qbB, pQBv, w=PS(bQB) + ["qbB"])
                    qa = qf.rearrange("p (h d) -> p h d", d=64)
                    tt("dve", qa, pQA.rearrange("p (h d) -> p h d", d=64),
                       sm[:, 2:10].unsqueeze(2).to_broadcast([128, 8, 64]), ALU.mult, r=["ssq"], w=PS(bQA) + ["qa"])
                    tt("dve", qa, qa, gq_bc.unsqueeze(1).to_broadcast([128, 8, 64]), ALU.mult, r=["qa", "gq"], w=["qa"])
                    if rope_on:
                        rope_blk(s, 0, qa, 8, ["qa"], [], qbA, "qbA", perm=True)
                    else:
                        cp("dve", qbA, qa.rearrange("p (g r) d -> p g r d", g=2), r=["qa"], w=["qbA"])

                def trQ(t):
                    s = t % 2
                    pTQ = bankbf(bTQ)
                    for j in range(8):
                        transpose(pTQ[:, j * 128:(j + 1) * 128], qb[:, j * 128:(j + 1) * 128], r=["qbA", "qbB"], w=PS(bTQ))
                    cp("act", QTt[s], pTQ.rearrange("p (a b) -> p a b", b=128), w=PS(bTQ) + [("QTt", s)])
                    dma("sp", qT_scr[:, :, tok0 + t * 128: tok0 + (t + 1) * 128], QTt[s], r=[("QTt", s)], w=["qT_scr"])

                def postK(t):
                    s = t % 2
                    kt = nctx // 128 + t
                    pX = bank(bX)
                    pYv = bank(bY).rearrange("p (b d) -> p b d", d=64)
                    act(sqt[:, 512:640], pX[:, 0:128], AF.Square, w=PS(bX) + ["sqk"])
                    redsum(sm[:, 10:12], sqt[:, 512:640].rearrange("p (h d) -> p h d", d=64), r=["sqk"], w=["ssk"])
                    rstd_small(sm[:, 10:12], 2, 1.0 / HD)(["ssk"], ["ssk"])
                    if rope_on:
                        rope_blk(s, 1, pYv, 8, [], PS(bY), kb3[:, 2:10, :], "kbB")
                    akn = kvf[:, 0:128].rearrange("p (g d) -> p g d", d=64)
                    tt("dve", akn, pX[:, 0:128].rearrange("p (g d) -> p g d", d=64),
                       sm[:, 10:12].unsqueeze(2).to_broadcast([128, 2, 64]), ALU.mult, r=["ssk"], w=PS(bX) + ["akn"])
                    tt("dve", akn, akn, gk_bc.unsqueeze(1).to_broadcast([128, 2, 64]), ALU.mult, r=["akn", "gk"], w=["akn"])
                    cp("act", V_A4[:, kt, :, 0:64], pX[:, 128:256].rearrange("p (g d) -> p g d", d=64),
                       r=["VA_ones"], w=PS(bX) + [("VA", kt)])
                    cp("act", V_B[:, kt, :], bank(bZ), w=PS(bZ) + [("VB", kt)])
                    if rope_on:
                        rope_blk(s, 0, akn, 2, ["akn"], [], kb3[:, 0:2, :], "kbA")
                    else:
                        cp("dve", kb[:, 0:128], kvf[:, 0:128], r=["akn"], w=["kbA"])
                        cp("dve", kb[:, 128:640], bank(bY), w=PS(bY) + ["kbB"])
                        cp("dve", kvf[:, 128:256], pX[:, 128:256], w=PS(bX) + ["kvf_v"])
                        cp("act", kvf[:, 256:768], bank(bY), w=PS(bY) + ["kvf_bk"])
                        cp("dve", kvf[:, 768:1280], bank(bZ), w=PS(bZ) + ["kvf_bv"])
                        prow = t * 128
                        dma("sp", nkv[prow:prow + 128, :], kvf, r=["akn", "kvf_v", "kvf_bk", "kvf_bv"])

                def trK(t):
                    kt = nctx // 128 + t
                    pTK = bankbf(bTK)
                    for j in range(5):
                        transpose(pTK[:, j * 128:(j + 1) * 128], kb[:, j * 128:(j + 1) * 128], r=["kbA", "kbB"], w=PS(bTK))
                    cp("act", KT_A[:, kt * 128:(kt + 1) * 128], pTK[:, 0:128], w=PS(bTK) + [("KTA", kt)])
                    cp("act", KT_B[:, :, kt * 128:(kt + 1) * 128],
                       pTK[:, 128:640].rearrange("p (h k) -> p h k", k=128), w=PS(bTK) + [("KTB", kt)])

                if "A" in phases:
                    loadx(0)
                    if ntile > 1:
                        loadx(1)
                    normAD(0)
                    tr_h(0)
                    loadrope(0)
                    if ntile > 1:
                        normAD(1)
                    projQ(0)
                    postQ(0)
                    for t in range(ntile):
                        if t + 2 < ntile:
                            loadx(t + 2)
                        if t + 1 < ntile:
                            loadrope(t + 1)
                            tr_h(t + 1)
                        projK(t)
                        if t + 2 < ntile:
                            normAD(t + 2)
                        if t >= 1:
                            trK(t - 1)
                        trQ(t)
                        postK(t)
                        if t + 1 < ntile:
                            projQ(t + 1)
                            postQ(t + 1)
                    trK(ntile - 1)
                S.barrier()
                AR.release(mA)

                if "B" not in phases:
                    continue
                mB = AR.mark()
                if kind == "s":
                    QC = 512
                    chunks = [dict(t0=q * QC, kts=list(range(NKT))) for q in range(ntok // QC)]
                else:
                    QC = SEQ
                    chunks = [dict(t0=p * SEQ, kts=list(range(p * (SEQ // 128), (p + 1) * (SEQ // 128)))) for p in range(NP)]
                nqc = len(chunks)
                QT = [AR.bf16(8, QC) for _ in range(2)]
                PT = [AR.bf16(2 * QC) for _ in range(3)]
                aoT = AR.bf16(8, QC)
                boT = AR.bf16(4, QC)
                NS = 1 if kind == "s" else 3
                EB = []
                for si in range(NS):
                    EB.append(dict(OsbA=[AR.f32(QC) for _ in range(2)], Oc=[AR.f32(QC) for _ in range(2)],
                                   rc=[AR.f32(QC) for _ in range(2)], dd=AR.f32(QC), d2=AR.bf16(QC), rs=AR.f32(QC)))
                Zacc = AR.f32(QC)
                pair_seq = [0]
                pairs = [dict(mix="A", r=r_) for r_ in range(4)] + [dict(mix="B", h=h) for h in range(4)]
                items = [(qc, pi, kt) for qc in range(nqc) for pi in range(len(pairs)) for kt in chunks[qc]["kts"]]

                def loadq(qc):
                    q0 = chunks[qc]["t0"]
                    dma("sp", QT[qc % 2], qT_scr[:, :, tok0 + q0: tok0 + q0 + QC], r=["qT_scr"], w=[("QT", qc % 2)])

                def emit_S(idx):
                    qc, pi, kt = items[idx]
                    if pi == 0 and kt == chunks[qc]["kts"][0] and qc + 1 < nqc:
                        loadq(qc + 1)
                    m = pairs[pi]
                    slot = idx % 2
                    bks = PS(2 * slot, 2 * slot + 1)
                    Q = QT[qc % 2]
                    for j in range(2):
                        base = j * 64
                        if m["mix"] == "A":
                            qT = Q[base:base + 64, m["r"], :]
                            kT = KT_A[base:base + 64, kt * 128:(kt + 1) * 128]
                        else:
                            qT = Q[base:base + 64, 4 + m["h"], :]
                            kT = KT_B[base:base + 64, m["h"], kt * 128:(kt + 1) * 128]
                        mm(bank(2 * slot + j)[:, 0:QC], kT, qT, True, True, r=[("QT", qc % 2)], w=bks)
                    src = ps[:, slot * 1024:(slot + 1) * 1024].rearrange("p (j q) -> p j q", j=2)[:, :, 0:QC]
                    act(PT[idx % 3].rearrange("p (j q) -> p j q", j=2), src, AF.Exp, w=bks + [("PT", idx % 3)], scale=HD ** -0.5)

                def emit_PV(idx):
                    qc, pi, kt = items[idx]
                    m = pairs[pi]
                    slot = idx % 3
                    first = kt == chunks[qc]["kts"][0]
                    last = kt == chunks[qc]["kts"][-1]
                    if m["mix"] == "A":
                        for j in range(2):
                            mm(bank(4 + j)[0:65, 0:QC], V_A4[:, kt, j, :], PT[slot][:, j * QC:(j + 1) * QC], first, last,
                               r=[("PT", slot)], w=PS(4 + j))
                    else:
                        h = m["h"]
                        for j in range(2):
                            mm(bank(4 + j)[:, 0:QC], V_B[:, kt, h * 128:(h + 1) * 128], PT[slot][:, j * QC:(j + 1) * QC], first, last,
                               r=[("PT", slot)], w=PS(4 + j))
                        mm(bank(6)[:, 0:QC], ones_bf, PT[slot][:, 0:QC], first, last, r=[("PT", slot), "ones_bf"], w=PS(6))
                        if first:
                            cp("dve", Zacc, PT[slot][:, QC:2 * QC], r=[("PT", slot)], w=["Zacc"])
                        else:
                            tt("dve", Zacc, Zacc, PT[slot][:, QC:2 * QC], ALU.add, r=[("PT", slot), "Zacc"], w=["Zacc"])
                    if last:
                        si = pair_seq[0] % NS
                        pair_seq[0] += 1
                        flush_set(si)
                        epilogue_p1(qc, pi, si)
                        defer_epilogue(qc, pi, si)

                pending = []

                def recip_any(pi, out, in_, r=(), w=()):
                    if kind != "s" or pi in (3, 4, 5, 6):
                        act(out, in_, AF.Ln, r=list(r), w=list(w))
                        act(out, out, AF.Exp, r=list(w), w=list(w), scale=-1.0)
                    else:
                        recip(out, in_, r=r, w=w)

                def epilogue_p1(qc, pi, si):
                    m = pairs[pi]
                    B_ = EB[si]
                    if m["mix"] == "A":
                        for j in range(2):
                            cp("dve", B_["OsbA"][j][0:65, :], bank(4 + j)[0:65, 0:QC], w=PS(4 + j) + [("OsbA", si, j)])
                    else:
                        mm(bank(7)[:, 0:QC], ones_f, Zacc, True, True, r=["Zacc", "ones_f"], w=PS(7))
                        for c in range(2):
                            cp("dve", B_["Oc"][c], bank(4 + c)[:, 0:QC], w=PS(4 + c) + [("Oc", si, c)])
                        cp("dve", B_["rs"], bank(6)[:, 0:QC], w=PS(6) + [("rs", si)])
                        recip_any(pi, B_["rc"][1], bank(7)[:, 0:QC], w=PS(7) + [("rc", si, 1)])

                def ep_A2(qc, pi, si, j):
                    m = pairs[pi]
                    B_ = EB[si]
                    hq = j * 4 + m["r"]
                    osb = B_["OsbA"][j]
                    rcj = B_["rc"][j]
                    ko = ("OsbA", si, j)
                    mm(bank(7)[0:64, 0:QC], sel65[0:65, :], osb[0:65, :], True, True, r=[ko, "sel65"], w=PS(7))
                    recip_any(pi, rcj[0:64, :], bank(7)[0:64, 0:QC], w=PS(7) + [("rc", si, j)])
                    tt("dve", aoT[0:64, hq, :], osb[0:64, :], rcj[0:64, :], ALU.mult, r=[ko, ("rc", si, j)], w=["aoT"])

                def ep_B2a(qc, pi, si):
                    B_ = EB[si]
                    Oc_, rc_, dd_, d2_ = B_["Oc"], B_["rc"], B_["dd"], B_["d2"]
                    recip_any(pi, rc_[0], B_["rs"], r=[("rs", si)], w=[("rc", si, 0)])
                    for c in range(2):
                        tt("dve", Oc_[c], Oc_[c], rc_[c], ALU.mult, r=[("rc", si, c), ("Oc", si, c)], w=[("Oc", si, c)])
                    stt("dve", dd_, Oc_[1], neg_lam, Oc_[0], ALU.mult, ALU.add,
                        r=[("Oc", si, 0), ("Oc", si, 1), "neg_lam"], w=[("dd", si)])
                    tt("dve", d2_, dd_, dd_, ALU.mult, r=[("dd", si)], w=[("d2", si)])

                def ep_B2b(qc, pi, si):
                    B_ = EB[si]
                    h = pairs[pi]["h"]
                    rs_ = B_["rs"]
                    mm(bank(7)[:, 0:QC], ones_bf, B_["d2"], True, True, r=[("d2", si), "ones_bf"], w=PS(7))
                    act(rs_, bank(7)[:, 0:QC], AF.Ln, r=["eps"], w=PS(7) + [("rs", si)], scale=1.0 / 128, bias=eps_t)
                    act(rs_, rs_, AF.Exp, r=[("rs", si)], w=[("rs", si)], scale=-0.5)
                    stt("dve", boT[:, h, :], B_["dd"], gs08, rs_, ALU.mult, ALU.mult, r=[("dd", si), ("rs", si), "gs08"], w=["boT"])
                    if pi == len(pairs) - 1:
                        q0 = chunks[qc]["t0"]
                        dma("sp", ao_scr[:, :, tok0 + q0: tok0 + q0 + QC], aoT[0:64, :, :], r=["aoT"], w=["ao_scr"])
                        dma("sp", bo_scr[:, :, tok0 + q0: tok0 + q0 + QC], boT, r=["boT"], w=["bo_scr"])

                def defer_epilogue(qc, pi, si):
                    if pairs[pi]["mix"] == "A":
                        pending.append([3, si, lambda: ep_A2(qc, pi, si, 0)])
                        pending.append([8, si, lambda: ep_A2(qc, pi, si, 1)])
                    else:
                        pending.append([3, si, lambda: ep_B2a(qc, pi, si)])
                        pending.append([11, si, lambda: ep_B2b(qc, pi, si)])

                def flush_set(si):
                    last_i = -1
                    for i_, p_ in enumerate(pending):
                        if p_[1] == si:
                            last_i = i_
                    for _ in range(last_i + 1):
                        pending.pop(0)[2]()

                def run_pending(force=False):
                    while pending and (force or pending[0][0] <= 0):
                        pending.pop(0)[2]()
                    for p_ in pending:
                        p_[0] -= 1

                loadq(0)
                sqi = seqs.index(sq)
                if sqi + 1 < len(seqs) and seqs[sqi + 1]["row"] != cur_row[0] and "A" in phases:
                    nrow = seqs[sqi + 1]["row"]
                    load_gm(0, nrow, 1, norm1_g)
                    load_bc(1, nrow, 0)
                    cur_row[0] = nrow
                if sqi == len(seqs) - 1 and "X" in phases:
                    load_bc(2, 0, 2)
                if kind == "s" and ("X" in phases or "C" in phases):
                    for (dst, src) in ((wg_bf, w_gate), (wba_bf, w_br_a), (wbb_bf, w_br_b), (wo_bf, w_out), (w1_bf, w_fc1), (w2_bf, w_fc2)):
                        for r0 in range(0, src.shape[0], 128):
                            dma("pool", dst[r0:r0 + 128, :], src[r0:r0 + 128, :], w=["wcast"])
                n_it = len(items)
                for idx in range(n_it + 2):
                    if idx < n_it:
                        emit_S(idx)
                    run_pending()
                    if idx >= 2:
                        emit_PV(idx - 2)
                run_pending(force=True)
                S.barrier()
                AR.release(mB)
            S.barrier()
            AR.release(mAB)
        AR.release(m_w_in)

        def groups(gsz):
            out = []
            for t0 in range(0, DS, gsz):
                out.append((t0, gsz, 0))
            tp = NP * SEQ
            t0 = 0
            while t0 < tp:
                n = min(gsz, tp - t0)
                out.append((DS + t0, n, 1))
                t0 += n
            return out

        if "X" in phases:
            mX = AR.mark()
            wg_sb = AR.bf16(8, 2048)
            wba_sb = AR.bf16(4, D)
            wbb_sb = AR.bf16(4, D)
            wo_sb = AR.bf16(8, D)
            wg_r = wg_bf.rearrange("(kc p) n -> p kc n", p=128)
            for kc in range(0, 8, 2):
                dma("act", wg_sb[:, kc:kc + 2, :], wg_r[:, kc:kc + 2, :], w=["wg"])
            dma("pool", wba_sb, wba_bf.rearrange("(h p) n -> p h n", p=128), r=["wg"], w=["wba"])
            dma("pool", wbb_sb, wbb_bf.rearrange("(h p) n -> p h n", p=128), r=["wg"], w=["wbb"])
            dma("pool", wo_sb, wo_bf.rearrange("(kc p) n -> p kc n", p=128), r=["wg"], w=["wo"])
            if "C" in phases:
                load_gm(0, 0, 4, norm2_g)
                load_bc(1, 0, 3)
                dma("sp", bc[3], final_norm_g[0, :].partition_broadcast(128), w=[BK[3]])
            GS = 512
            hTg = [AR.bf16(8, GS) for _ in range(2)]
            aoG = [AR.bf16(4, GS) for _ in range(2)]
            boG = [AR.bf16(4, GS) for _ in range(2)]
            gT = AR.bf16(16, GS)
            mT = AR.bf16(8, GS)
            t1 = [AR.f32(GS) for _ in range(2)]
            t2 = [AR.f32(GS) for _ in range(2)]
            xg = [AR.f32(D) for _ in range(2)]
            tx = [AR.f32(D) for _ in range(2)]
            grp = groups(GS)
            cur = [0 if ("B" in phases) else None]

            def loadg(gi):
                t0, n, row = grp[gi]
                s = gi % 2
                dma("sp", hTg[s][:, :, 0:n], hT_scr[:, :, t0:t0 + n], w=[("hTg", s)])
                ao_v = ao_scr.rearrange("d (j two) t -> d j two t", two=2)
                dma("sp", aoG[s][0:64, :, 0:n], ao_v[:, :, 0, t0:t0 + n], w=[("aoG", s, 0)])
                dma("sp", aoG[s][64:128, :, 0:n], ao_v[:, :, 1, t0:t0 + n], w=[("aoG", s, 1)])
                dma("sp", boG[s][:, :, 0:n], bo_scr[:, :, t0:t0 + n], w=[("boG", s)])

            loadg(0)
            xi = 0
            for gi, (t0, n, row) in enumerate(grp):
                if gi + 1 < len(grp):
                    loadg(gi + 1)
                s = gi % 2
                if cur[0] != row:
                    load_bc(2, row, 2)
                    cur[0] = row
                for oc in range(16):
                    pb = oc % 2
                    for kc in range(8):
                        mm(bank(pb)[:, 0:n], wg_sb[:, kc, oc * 128:(oc + 1) * 128], hTg[s][:, kc, 0:n], kc == 0, kc == 7,
                           r=["wg", ("hTg", s)], w=PS(pb))
                    act(gT[:, oc, 0:n], bank(pb)[:, 0:n], AF.Sigmoid, r=["bgT"], w=PS(pb) + [("gT", oc)], bias=bgT[:, oc:oc + 1])
                for oc in range(8):
                    pa = 2 + 2 * (oc % 2)
                    pbb = pa + 1
                    for j in range(4):
                        mm(bank(pa)[:, 0:n], wba_sb[:, j, oc * 128:(oc + 1) * 128], aoG[s][:, j, 0:n], j == 0, j == 3,
                           r=["wba", ("aoG", s, 0), ("aoG", s, 1)], w=PS(pa))
                    for h in range(4):
                        mm(bank(pbb)[:, 0:n], wbb_sb[:, h, oc * 128:(oc + 1) * 128], boG[s][:, h, 0:n], h == 0, h == 3,
                           r=["wbb", ("boG", s)], w=PS(pbb))
                    tt("dve", t1[oc % 2][:, 0:n], bank(pa)[:, 0:n], gT[:, oc, 0:n], ALU.mult, r=[("gT", oc)], w=PS(pa) + [("t1", oc % 2)])
                    tt("dve", t2[oc % 2][:, 0:n], bank(pbb)[:, 0:n], gT[:, 8 + oc, 0:n], ALU.mult, r=[("gT", 8 + oc)], w=PS(pbb) + [("t2", oc % 2)])
                    tt("pool", mT[:, oc, 0:n], t1[oc % 2][:, 0:n], t2[oc % 2][:, 0:n], ALU.add,
                       r=[("t1", oc % 2), ("t2", oc % 2)], w=[("mT", oc)])
                for i in range(n // 128):
                    sx = xi % 2
                    xi += 1
                    pbs = (6, 7) if sx == 0 else (0, 1)
                    dma("sp", xg[sx], x_all[t0 + i * 128: t0 + (i + 1) * 128, :], w=[("xg", sx)])
                    for hf in range(2):
                        for kc in range(8):
                            mm(bank(pbs[hf]), mT[:, kc, i * 128:(i + 1) * 128], wo_sb[:, kc, hf * 512:(hf + 1) * 512], kc == 0, kc == 7,
                               r=["wo", ("mT", kc)], w=PS(pbs[hf]))
                    for hf in range(2):
                        tt("dve", tx[sx][:, hf * 512:(hf + 1) * 512], bank(pbs[hf]), bc[2][:, hf * 512:(hf + 1) * 512], ALU.mult,
                           r=[BK[2]], w=PS(pbs[hf]) + [("tx", sx, hf)])
                    tt("pool", tx[sx], tx[sx], xg[sx], ALU.add, r=[("tx", sx, 0), ("tx", sx, 1), ("xg", sx)], w=[("tx", sx, 0), ("tx", sx, 1)])
                    dma("sp", x1_scr[t0 + i * 128: t0 + (i + 1) * 128, :], tx[sx], r=[("tx", sx, 0), ("tx", sx, 1)], w=["x1_scr"])
            S.barrier()
            AR.release(mX)

        if "C" in phases:
            mC = AR.mark()
            w1_sb = AR.bf16(8, 4096)
            w2_sb = AR.bf16(32, D)
            w1_r = w1_bf.rearrange("(kc p) n -> p kc n", p=128)
            w2_r = w2_bf.rearrange("(fc p) n -> p fc n", p=128)
            for kc in range(8):
                dma("act", w1_sb[:, kc, :], w1_r[:, kc, :], w=["w1"])
            for f4 in range(8):
                dma("pool", w2_sb[:, f4 * 4:(f4 + 1) * 4, :], w2_r[:, f4 * 4:(f4 + 1) * 4, :], r=["w1"], w=["w2"])
            GS = 256
            NX1 = 4
            x1t = [AR.f32(D) for _ in range(NX1)]
            hbs = [AR.bf16(D) for _ in range(2)]
            h2T = [AR.bf16(8, GS) for _ in range(2)]
            rr = [AR.f32(2 * GS) for _ in range(2)]
            uT = AR.bf16(32, GS)
            ty = [AR.f32(D) for _ in range(2)]
            smc = AR.f32(8)
            grp = groups(GS)
            cur = [None]
            c_pref = "X" in phases
            if not c_pref:
                dma("sp", bc[3], final_norm_g[0, :].partition_broadcast(128), w=[BK[3]])
            xslots = {}
            xi = [0]

            def prep(gi):
                t0, n, row = grp[gi]
                if cur[0] != row:
                    if not (c_pref and cur[0] is None and row == 0):
                        load_gm(0, row, 4, norm2_g)
                        load_bc(1, row, 3)
                    load_bc(2, row, 5)
                    cur[0] = row
                sl = []
                for i in range(n // 128):
                    sx = xi[0] % NX1
                    sy = xi[0] % 2
                    xi[0] += 1
                    sl.append((sx, sy))
                    dma("sp", x1t[sx], x1_scr[t0 + i * 128: t0 + (i + 1) * 128, :], r=["x1_scr"], w=[("x1t", sx)])
                    norm_to_bf16(x1t[sx], [("x1t", sx)], hbs[i], ty[sy], [("ty", sy, 0), ("ty", sy, 1)], smc[:, 2 + i:3 + i], ("ssC", i), hbs[i], ("hb", i))
                xslots[gi] = sl

            def prep_tr(gi):
                t0, n, row = grp[gi]
                hT_ = h2T[gi % 2]
                for i in range(n // 128):
                    pT = bankbf(i)
                    for kc in range(8):
                        transpose(pT[:, kc * 128:(kc + 1) * 128], hbs[i][:, kc * 128:(kc + 1) * 128], r=[("hb", i)], w=PS(i))
                    cp("act", hT_[:, :, i * 128:(i + 1) * 128], pT.rearrange("p (a b) -> p a b", b=128), w=PS(i) + [("h2T", gi % 2)])

            def fc1(gi):
                t0, n, row = grp[gi]
                hT_ = h2T[gi % 2]
                for fp in range(16):
                    pb = 2 + (fp % 2)
                    for k in range(2):
                        fc = fp * 2 + k
                        for kc in range(8):
                            mm(bank(pb)[:, k * GS: k * GS + n], w1_sb[:, kc, fc * 128:(fc + 1) * 128], hT_[:, kc, 0:n], kc == 0, kc == 7,
                               r=["w1", ("h2T", gi % 2)], w=PS(pb))
                    rv = rr[fp % 2].rearrange("p (k t) -> p k t", k=2)
                    act(rv[:, :, 0:n], bank(pb).rearrange("p (k t) -> p k t", k=2)[:, :, 0:n], AF.Relu, w=PS(pb) + [("rr", fp % 2)])
                    tt("dve", uT[:, fp * 2:fp * 2 + 2, 0:n], rv[:, :, 0:n], rv[:, :, 0:n], ALU.mult, r=[("rr", fp % 2)], w=[("uT", fp)])

            def fc2(gi, mid=None):
                t0, n, row = grp[gi]
                for i in range(n // 128):
                    if i == 1 and mid is not None:
                        mid()
                    sx, sy = xslots[gi][i]
                    pbs = (4, 5) if i % 2 == 0 else (6, 7)
                    for hf in range(2):
                        for fc in range(32):
                            mm(bank(pbs[hf]), uT[:, fc, i * 128:(i + 1) * 128], w2_sb[:, fc, hf * 512:(hf + 1) * 512], fc == 0, fc == 31,
                               r=["w2", ("uT", fc // 2)], w=PS(pbs[hf]))
                    for hf in range(2):
                        tt("dve", ty[sy][:, hf * 512:(hf + 1) * 512], bank(pbs[hf]), bc[2][:, hf * 512:(hf + 1) * 512], ALU.mult,
                           r=[BK[2]], w=PS(pbs[hf]) + [("ty", sy, hf)])
                    kty = [("ty", sy, 0), ("ty", sy, 1)]
                    tt("pool", ty[sy], ty[sy], x1t[sx], ALU.add, r=kty + [("x1t", sx)], w=kty)
                    act(x1t[sx], ty[sy], AF.Square, r=kty, w=[("x1t", sx), "ssF"], accum=smc[:, 1:2])
                    rstd_small(smc[:, 1:2], 1, 1.0 / D)(["ssF"], ["ssF"])
                    stt("dve", ty[sy], ty[sy], smc[:, 1:2], bc[3], ALU.mult, ALU.mult, r=kty + ["ssF", BK[3]], w=kty)
                    dma("sp", y_all[t0 + i * 128: t0 + (i + 1) * 128, :], ty[sy], r=kty)

            prep(0)
            prep_tr(0)
            for gi in range(len(grp)):
                has_next = gi + 1 < len(grp)
                nxt_same = has_next and grp[gi + 1][2] == grp[gi][2]
                fc1(gi)
                if nxt_same:
                    prep(gi + 1)
                    if grp[gi][1] >= 256:
                        fc2(gi, mid=lambda g=gi + 1: prep_tr(g))
                    else:
                        fc2(gi)
                        prep_tr(gi + 1)
                else:
                    fc2(gi)
                    if has_next:
                        prep(gi + 1)
                        prep_tr(gi + 1)
            AR.release(mC)

        run = S.emit(eng_sems, dma_sems)
        with nc.Block() as block:
            @block.sync
            def _(e):
                run("sp", e)

            @block.scalar
            def _(e):
                run("act", e)

            @block.tensor
            def _(e):
                run("pe", e)

            @block.vector
            def _(e):
                run("dve", e)

            @block.gpsimd
            def _(e):
                run("pool", e)
    build_program.last_stats = dict(S.stats, arena_peak_words=AR.peak)
    return nc


def make_in_maps(inputs, n_cores, NP, SEQ, DS, PAST):
    f = lambda a: np.ascontiguousarray(np.asarray(a, dtype=np.float32))
    xp = f(inputs["x_prompt"])
    xs = f(inputs["x_sample"])
    shared = {
        "w_mod": f(inputs["w_mod"][0]), "b_mod": f(inputs["b_mod"]), "norm1_g": f(inputs["norm1_g"]),
        "w_in": f(inputs["w_in"][0]), "a_q_norm_g": f(inputs["a_q_norm_g"]), "a_k_norm_g": f(inputs["a_k_norm_g"]),
        "lam_q1": f(inputs["lam_q1"]), "lam_k1": f(inputs["lam_k1"]), "lam_q2": f(inputs["lam_q2"]), "lam_k2": f(inputs["lam_k2"]),
        "b_subln_g": f(inputs["b_subln_g"]), "w_gate": f(inputs["w_gate"][0]), "b_gate": f(inputs["b_gate"]),
        "w_br_a": f(inputs["w_br_a"][0]), "w_br_b": f(inputs["w_br_b"][0]), "w_out": f(inputs["w_out"][0]),
        "norm2_g": f(inputs["norm2_g"]), "w_fc1": f(inputs["w_fc1"][0]), "w_fc2": f(inputs["w_fc2"][0]),
        "final_norm_g": f(inputs["final_norm_g"]).reshape(1, D),
        "ident": np.eye(128, dtype=np.float32),
    }
    maps = []
    for c in range(n_cores):
        m = dict(shared)
        m["x_all"] = np.ascontiguousarray(np.concatenate([xs[c], xp[c * NP:(c + 1) * NP].reshape(NP * SEQ, D)], axis=0))
        m["cak"] = f(inputs["cache_a_k"][c, 0]).reshape(PAST, 128)
        m["cav"] = f(inputs["cache_a_v"][c, 0]).reshape(PAST, 128)
        m["cbk"] = f(inputs["cache_b_k"][c, 0]).reshape(PAST, 512)
        m["cbv"] = f(inputs["cache_b_v"][c, 0]).reshape(PAST, 512)
        m["cvec"] = np.ascontiguousarray(np.stack([f(inputs["c"])[c], f(inputs["c_ctx"])], axis=0))
        maps.append(m)
    return maps


def assemble(results, n_cores, NP, SEQ, DS):
    ys = np.stack([r["y_all"][:DS] for r in results], axis=0)
    yp = np.concatenate([r["y_all"][DS:].reshape(NP, SEQ, D) for r in results], axis=0)
    nkv = np.concatenate([r["nkv"].reshape(NP, SEQ, 1280) for r in results], axis=0)
    B = n_cores * NP
    nak = nkv[:, :, 0:128].reshape(B, 1, SEQ, 2, 64)
    nav = nkv[:, :, 128:256].reshape(B, 1, SEQ, 2, 64)
    nbk = nkv[:, :, 256:768].reshape(B, 1, SEQ, 4, 2, 64)
    nbv = nkv[:, :, 768:1280].reshape(B, 1, SEQ, 4, 128)
    c = lambda a: np.ascontiguousarray(a.astype(np.float32))
    return (c(yp), c(ys), c(nak), c(nav), c(nbk), c(nbv))


def kernel(**inputs):
    NP, SEQ, DS, PAST = 4, 256, 4096, 512
    nc = build_program(NP=NP, SEQ=SEQ, DS=DS, PAST=PAST)
    in_maps = make_in_maps(inputs, N_CORES, NP, SEQ, DS, PAST)
    res = run_bass_kernel_spmd(nc, in_maps, core_ids=list(range(N_CORES)))
    return assemble(res.results, N_CORES, NP, SEQ, DS)
```

```python
import numpy as np
import concourse.bass as bass
import concourse.mybir as mybir
from concourse.bass_utils import run_bass_kernel_spmd

F32 = mybir.dt.float32
BF16 = mybir.dt.bfloat16
AF = mybir.ActivationFunctionType
ALU = mybir.AluOpType
AX = mybir.AxisListType

D = 1024
HD = 64
EPS = 1e-6
LAM_INIT = 0.2
N_CORES = 8


class _Op:
    __slots__ = ("id", "eng", "fn", "deps", "dma", "sem", "skey", "val", "has_dep")

    def __init__(self, id, eng, fn, dma):
        self.id = id
        self.eng = eng
        self.fn = fn
        self.deps = {}
        self.dma = dma
        self.sem = None
        self.skey = None
        self.val = 0
        self.has_dep = False


class Sched:
    ENGS = ("sp", "act", "pe", "dve", "pool")
    NDMA = 12

    def __init__(self):
        self.ops = []
        self.last_w = {}
        self.readers = {}
        self.eng_ops = {e: [] for e in self.ENGS}
        self.dma_hist = {e: [] for e in self.ENGS}
        self.bar = ()

    def add(self, eng, fn, r=(), w=(), dma=False):
        op = _Op(len(self.ops), eng, fn, dma)
        deps = op.deps
        for b in self.bar:
            deps[b] = True
        for k in r:
            lw = self.last_w.get(k)
            if lw is not None:
                deps[lw] = True
        for k in w:
            lw = self.last_w.get(k)
            if lw is not None:
                deps.setdefault(lw, False)
            for rd in self.readers.get(k, ()):
                deps.setdefault(rd, False)
        for k in r:
            self.readers.setdefault(k, []).append(op.id)
        for k in w:
            self.last_w[k] = op.id
            self.readers[k] = []
        if dma:
            h = self.dma_hist[eng]
            if len(h) >= self.NDMA:
                deps.setdefault(h[-self.NDMA], True)
            h.append(op.id)
        self.ops.append(op)
        self.eng_ops[eng].append(op)
        return op

    def barrier(self):
        bar = []
        for e in self.ENGS:
            lst = self.eng_ops[e]
            last_c = None
            for op in reversed(lst):
                if not op.dma:
                    last_c = op.id
                    break
            if last_c is not None:
                bar.append(last_c)
            bar.extend(self.dma_hist[e][-self.NDMA:])
        self.bar = tuple(bar)
        self.last_w = {}
        self.readers = {}

    @staticmethod
    def _needs_wait(cons, prod, is_raw):
        if prod.dma or prod.eng != cons.eng:
            return True
        if cons.dma:
            return True
        if cons.eng == "pe":
            return False
        return True

    def emit(self, eng_sems, dma_sems):
        ops = self.ops
        for op in ops:
            keep = {}
            for d, raw in op.deps.items():
                p = ops[d]
                if self._needs_wait(op, p, raw):
                    keep[d] = raw
                    p.has_dep = True
            op.deps = keep
        cnt = {e: 0 for e in self.ENGS}
        dcnt = {e: [0] * self.NDMA for e in self.ENGS}
        dn = {e: 0 for e in self.ENGS}
        for op in ops:
            if op.dma:
                i = dn[op.eng] % self.NDMA
                dn[op.eng] += 1
                dcnt[op.eng][i] += 16
                op.sem = dma_sems[op.eng][i]
                op.skey = ("d", op.eng, i)
                op.val = dcnt[op.eng][i]
            elif op.has_dep:
                cnt[op.eng] += 1
                op.sem = eng_sems[op.eng]
                op.skey = ("e", op.eng)
                op.val = cnt[op.eng]
        self.stats = {"waits": 0, "incs": dict(cnt), "ops": len(ops)}
        finals = []
        for e in self.ENGS:
            for i in range(self.NDMA):
                if dcnt[e][i]:
                    finals.append((dma_sems[e][i], dcnt[e][i]))

        def run(engname, e):
            known = {}
            for op in self.eng_ops[engname]:
                need = {}
                for d in op.deps:
                    p = ops[d]
                    key = p.skey
                    if known.get(key, 0) < p.val and (key not in need or need[key][1] < p.val):
                        need[key] = (p.sem, p.val)
                for key, (s, v) in need.items():
                    e.wait_ge(s, v)
                    known[key] = v
                    self.stats["waits"] += 1
                ins = op.fn(e)
                if op.dma:
                    ins.then_inc(op.sem, 16)
                elif op.has_dep:
                    ins.then_inc(op.sem, 1)
            if engname == "sp":
                for s, v in finals:
                    e.wait_ge(s, v)

        return run


class Arena:
    def __init__(self, t, nwords):
        self.t = t
        self.n = nwords
        self.off = 0
        self.peak = 0

    def mark(self):
        return self.off

    def release(self, m):
        self.off = m

    def _take(self, words):
        words = (words + 7) // 8 * 8
        o = self.off
        self.off += words
        assert self.off <= self.n, f"arena overflow {self.off} > {self.n}"
        self.peak = max(self.peak, self.off)
        return o

    def f32(self, *shape, parts=128):
        n = int(np.prod(shape))
        o = self._take(n)
        ap = self.t[0:parts, o:o + n]
        return self._shape(ap, shape)

    def bf16(self, *shape, parts=128):
        n = int(np.prod(shape))
        o = self._take((n + 1) // 2)
        ap = self.t[0:parts, o:o + (n + 1) // 2].bitcast(BF16)[:, 0:n]
        return self._shape(ap, shape)

    @staticmethod
    def _shape(ap, shape):
        if len(shape) == 1:
            return ap
        if len(shape) == 2:
            return ap.rearrange("p (a b) -> p a b", b=shape[1])
        if len(shape) == 3:
            return ap.rearrange("p (a b c) -> p a b c", b=shape[1], c=shape[2])
        raise ValueError(shape)


def build_program(NP=4, SEQ=256, DS=4096, PAST=512, arena_words=53120, phases="MABXC"):
    assert DS % 512 == 0 and SEQ % 128 == 0 and PAST % 128 == 0
    TALL = DS + NP * SEQ
    nc = bass.Bass("TRN2", target_bir_lowering=False)

    def din(name, shape):
        return nc.dram_tensor(name, list(shape), F32, kind="ExternalInput").ap()

    x_all = din("x_all", [TALL, D])
    cak = din("cak", [PAST, 128])
    cav = din("cav", [PAST, 128])
    cbk = din("cbk", [PAST, 512])
    cbv = din("cbv", [PAST, 512])
    cvec = din("cvec", [2, D])
    w_mod = din("w_mod", [D, 6 * D])
    b_mod = din("b_mod", [1, 6 * D])
    norm1_g = din("norm1_g", [1, D])
    w_in = din("w_in", [D, 2304])
    a_q_norm_g = din("a_q_norm_g", [1, HD])
    a_k_norm_g = din("a_k_norm_g", [1, HD])
    lam_q1 = din("lam_q1", [1, HD])
    lam_k1 = din("lam_k1", [1, HD])
    lam_q2 = din("lam_q2", [1, HD])
    lam_k2 = din("lam_k2", [1, HD])
    b_subln_g = din("b_subln_g", [1, 128])
    w_gate = din("w_gate", [D, 2 * D])
    b_gate = din("b_gate", [1, 2 * D])
    w_br_a = din("w_br_a", [512, D])
    w_br_b = din("w_br_b", [512, D])
    w_out = din("w_out", [D, D])
    norm2_g = din("norm2_g", [1, D])
    w_fc1 = din("w_fc1", [D, 4 * D])
    w_fc2 = din("w_fc2", [4 * D, D])
    final_norm_g = din("final_norm_g", [1, D])
    ident_d = din("ident", [128, 128])

    y_all = nc.dram_tensor("y_all", [TALL, D], F32, kind="ExternalOutput").ap()
    nkv = nc.dram_tensor("nkv", [NP * SEQ, 1280], F32, kind="ExternalOutput").ap()

    hT_scr = nc.dram_tensor("hT_scr", [128, 8, TALL], BF16, kind="Internal").ap()
    rope_d = nc.dram_tensor("rope_scr", [DS, 128], F32, kind="Internal").ap()
    qT_scr = nc.dram_tensor("qT_scr", [128, 8, TALL], BF16, kind="Internal").ap()
    ao_scr = nc.dram_tensor("ao_scr", [64, 8, TALL], BF16, kind="Internal").ap()
    bo_scr = nc.dram_tensor("bo_scr", [128, 4, TALL], BF16, kind="Internal").ap()
    x1_scr = nc.dram_tensor("x1_scr", [TALL, D], F32, kind="Internal").ap()
    mods_scr = nc.dram_tensor("mods_scr", [2, 6 * D], F32, kind="Internal").ap()
    wg_bf = nc.dram_tensor("wg_bf", [D, 2 * D], BF16, kind="Internal").ap()
    wba_bf = nc.dram_tensor("wba_bf", [512, D], BF16, kind="Internal").ap()
    wbb_bf = nc.dram_tensor("wbb_bf", [512, D], BF16, kind="Internal").ap()
    wo_bf = nc.dram_tensor("wo_bf", [D, D], BF16, kind="Internal").ap()
    w1_bf = nc.dram_tensor("w1_bf", [D, 4 * D], BF16, kind="Internal").ap()
    w2_bf = nc.dram_tensor("w2_bf", [4 * D, D], BF16, kind="Internal").ap()

    S = Sched()

    def dma(q, out, in_, r=(), w=(), slow=False):
        if slow:
            S.add(q, lambda e: e.dma_start(out=out, in_=in_, allow_slow_non_contiguous=True), r=r, w=w, dma=True)
        else:
            S.add(q, lambda e: e.dma_start(out=out, in_=in_), r=r, w=w, dma=True)

    def mm(out, lhsT, rhs, start, stop, r=(), w=()):
        S.add("pe", lambda e: e.matmul(out, lhsT=lhsT, rhs=rhs, start=start, stop=stop), r=r, w=w)

    def act(out, in_, func, r=(), w=(), scale=1.0, bias=None, accum=None):
        def f(e):
            kw = {}
            if bias is not None:
                kw["bias"] = bias
            if accum is not None:
                kw["accum_out"] = accum
            return e.activation(out=out, in_=in_, func=func, scale=scale, **kw)
        S.add("act", f, r=r, w=w)

    def cp(eng, out, in_, r=(), w=()):
        if eng == "act":
            S.add("act", lambda e: e.copy(out=out, in_=in_), r=r, w=w)
        else:
            S.add(eng, lambda e: e.tensor_copy(out=out, in_=in_), r=r, w=w)

    def tt(eng, out, a, b, op, r=(), w=()):
        S.add(eng, lambda e: e.tensor_tensor(out=out, in0=a, in1=b, op=op), r=r, w=w)

    def stt(eng, out, in0, scalar, in1, op0, op1, r=(), w=()):
        S.add(eng, lambda e: e.scalar_tensor_tensor(out=out, in0=in0, scalar=scalar, in1=in1, op0=op0, op1=op1), r=r, w=w)

    def ts(eng, out, in0, s1, s2, op0, op1=None, r=(), w=()):
        def f(e):
            if op1 is None:
                return e.tensor_scalar(out=out, in0=in0, scalar1=s1, scalar2=None, op0=op0)
            return e.tensor_scalar(out=out, in0=in0, scalar1=s1, scalar2=s2, op0=op0, op1=op1)
        S.add(eng, f, r=r, w=w)

    def recip(out, in_, r=(), w=()):
        S.add("dve", lambda e: e.reciprocal(out=out, in_=in_), r=r, w=w)

    def redsum(out, in_, r=(), w=()):
        S.add("dve", lambda e: e.reduce_sum(out=out, in_=in_, axis=AX.X), r=r, w=w)

    def memset(eng, ap, val, w=()):
        S.add(eng, lambda e: e.memset(ap, val), w=w)

    from contextlib import ExitStack
    with ExitStack() as st:
        arena_t = st.enter_context(nc.sbuf_tensor("arena", [128, arena_words], F32))
        ps = st.enter_context(nc.psum_tensor("ps", [128, 4096], F32))
        eng_sems = {e: st.enter_context(nc.semaphore("s_" + e)) for e in Sched.ENGS}
        dma_sems = {e: [st.enter_context(nc.semaphore(f"d_{e}{i}")) for i in range(Sched.NDMA)]
                    for e in ("sp", "act", "pool")}
        for e in ("pe", "dve"):
            dma_sems[e] = [None] * Sched.NDMA
        AR = Arena(arena_t, arena_words)

        def bank(b, n=1):
            return ps[:, b * 512:(b + n) * 512]

        def bankbf(b):
            return bank(b).bitcast(BF16)

        def PS(*bs):
            return [("ps", b) for b in bs]

        def transpose(out, in_, r=(), w=()):
            S.add("pe", lambda e: e.transpose(out=out, in_=in_, identity=identb), r=list(r) + ["ident"], w=w)

        identb = AR.bf16(128)
        eps_t = AR.f32(1)
        ones_bf = AR.bf16(128)
        ones_row = AR.f32(128)
        ones_f = AR.f32(128)
        sel65 = AR.f32(64)
        neg_lam = AR.f32(1)
        gs08 = AR.f32(1)
        gk_bc = AR.f32(64)
        gq_bc = AR.f32(64)
        bgT = AR.f32(16)
        bc = [AR.f32(D) for _ in range(5)]
        BK = [("bc", i) for i in range(5)]
        m_w_in = AR.mark()
        w_in_sb = AR.bf16(8, 2304)
        w_in_r = w_in.rearrange("(kc p) n -> p kc n", p=128)
        for kc in range(8):
            dma("pool", w_in_sb[:, kc, :], w_in_r[:, kc, :], w=["w_in"])
        m_init = AR.mark()
        identf = AR.f32(128)
        lamtmp = AR.f32(4, 64)
        lamred = AR.f32(4)

        dma("sp", identf, ident_d, w=["identf"])
        cp("dve", identb, identf, r=["identf"], w=["ident"])
        memset("dve", eps_t, EPS, w=["eps"])
        memset("dve", ones_bf, 1.0, w=["ones_bf"])
        memset("dve", ones_row, 1.0, w=["ones_row"])
        memset("dve", ones_f, 1.0, w=["ones_f"])
        memset("dve", sel65, 0.0, w=["sel65"])
        memset("pool", sel65[64:65, :], 1.0, w=["sel65"])
        dma("sp", gk_bc, a_k_norm_g[0, :].partition_broadcast(128), w=["gk"])
        dma("sp", gq_bc, a_q_norm_g[0, :].partition_broadcast(128), w=["gq"])
        def late_init():
            dma("act", bgT, b_gate.rearrange("o (oc p) -> p (o oc)", p=128), w=["bgT"], slow=True)
            dma("act", gs08, b_subln_g.rearrange("o p -> p o"), w=["gs08"], slow=True)
            ts("dve", gs08, gs08, 1.0 - LAM_INIT, None, ALU.mult, r=["gs08"], w=["gs08"])
            for i, v in enumerate((lam_q1, lam_k1, lam_q2, lam_k2)):
                dma("act", lamtmp[:, i, :], v[0, :].partition_broadcast(128), w=[("lam", i)])
            tt("dve", lamtmp[:, 0, :], lamtmp[:, 0, :], lamtmp[:, 1, :], ALU.mult, r=[("lam", 0), ("lam", 1)], w=[("lam", 0)])
            tt("dve", lamtmp[:, 2, :], lamtmp[:, 2, :], lamtmp[:, 3, :], ALU.mult, r=[("lam", 2), ("lam", 3)], w=[("lam", 2)])
            redsum(lamred[:, 0:1], lamtmp[:, 0, :], r=[("lam", 0)], w=["lamred0"])
            redsum(lamred[:, 1:2], lamtmp[:, 2, :], r=[("lam", 2)], w=["lamred1"])
            act(lamred[:, 0:2], lamred[:, 0:2], AF.Exp, r=["lamred0", "lamred1"], w=["lamred0", "lamred1"])
            stt("dve", neg_lam, lamred[:, 1:2], -LAM_INIT, lamred[:, 0:1], ALU.add, ALU.subtract,
                r=["lamred0", "lamred1"], w=["neg_lam"])

            import math
            ntl = DS // 128
            pidx = AR.f32(1)
            r64 = AR.f32(1)
            colv = AR.f32(1)
            negpi = AR.f32(1)
            pospi = AR.f32(1)
            jidx = AR.f32(16)
            freqs = AR.f32(16)
            angc = AR.f32(16)
            rowv = AR.f32(ntl)
            ang = AR.f32(ntl, 32)
            msin = AR.f32(ntl, 32)
            mcos = AR.f32(ntl, 32)
            Ttab = AR.f32(ntl, 128)
            S.add("pool", lambda e: e.iota(pidx, [[0, 1]], base=0, channel_multiplier=1, allow_small_or_imprecise_dtypes=True), w=["pidx"])
            S.add("pool", lambda e: e.iota(jidx, [[1, 16]], base=0, channel_multiplier=0, allow_small_or_imprecise_dtypes=True), w=["jidx"])
            S.add("pool", lambda e: e.iota(rowv, [[2, ntl]], base=0, channel_multiplier=0, allow_small_or_imprecise_dtypes=True), w=["rowv"])
            memset("dve", negpi, -math.pi, w=["negpi"])
            memset("dve", pospi, math.pi, w=["pospi"])
            S.add("dve", lambda e: e.tensor_single_scalar(out=r64, in_=pidx, scalar=64.0, op=ALU.is_ge), r=["pidx"], w=["r64"])
            stt("dve", colv, r64, -64.0, pidx, ALU.mult, ALU.add, r=["r64", "pidx"], w=["colv"])
            act(freqs, jidx, AF.Exp, r=["jidx"], w=["freqs"], scale=-math.log(10000.0) / 16.0)
            ts("dve", rowv, rowv, r64, None, ALU.add, r=["rowv", "r64"], w=["rowv"])
            tt("dve", ang[:, :, 0:16], rowv.unsqueeze(2).to_broadcast([128, ntl, 16]), freqs.unsqueeze(1).to_broadcast([128, ntl, 16]),
               ALU.mult, r=["rowv", "freqs"], w=["ang_r"])
            ts("dve", angc, freqs, colv, None, ALU.mult, r=["freqs", "colv"], w=["angc"])
            cp("dve", ang[:, :, 16:32], angc.unsqueeze(1).to_broadcast([128, ntl, 16]), r=["angc"], w=["ang_c"])
            kacc = AR.f32(ntl, 32)
            maxang = float(max(DS // 64 - 1, 63)) + 1.5 * math.pi
            nthr = int(maxang / (2.0 * math.pi)) + 1

            def reduce_angle(dst, kdst, shift):
                ts("dve", dst, ang, shift, None, ALU.add, r=["ang_r", "ang_c"], w=[kdst])
                memset("dve", kacc, 0.0, w=["kacc"])
                for i in range(1, nthr + 1):
                    stt("dve", kacc, dst, (2 * i - 1) * math.pi, kacc, ALU.is_ge, ALU.add, r=[kdst, "kacc"], w=["kacc"])
                stt("dve", dst, kacc, -2.0 * math.pi, dst, ALU.mult, ALU.add, r=["kacc", kdst], w=[kdst])

            reduce_angle(msin, "msin", 0.0)
            reduce_angle(mcos, "mcos", 0.5 * math.pi)
            act(Ttab[:, :, 0:32], mcos, AF.Sin, r=["mcos"], w=["T0"])
            act(Ttab[:, :, 96:128], msin, AF.Sin, r=["msin"], w=["T3"])
            act(Ttab[:, :, 64:96], msin, AF.Sin, r=["msin"], w=["T2"], scale=-1.0)
            cp("dve", Ttab[:, :, 32:64], Ttab[:, :, 0:32], r=["T0"], w=["T1"])
            dma("act", rope_d.rearrange("(t p) c -> p t c", p=128), Ttab, r=["T0", "T1", "T2", "T3"], w=["rope_scr"])

        def load_bc(i, row, j):
            dma("sp", bc[i], mods_scr[row, j * D:(j + 1) * D].partition_broadcast(128), r=["mods_scr"], w=[BK[i]])

        def load_gm(i, row, jscale, g_dram):
            load_bc(4, row, jscale)
            dma("sp", bc[i], g_dram[0, :].partition_broadcast(128), w=[BK[i]])
            stt("dve", bc[i], bc[4], 1.0, bc[i], ALU.add, ALU.mult, r=[BK[4], BK[i]], w=[BK[i]])

        if "M" not in phases:
            late_init()
            S.barrier()
            AR.release(m_init)
        if "M" in phases:
            m0 = m_init
            cT = AR.f32(2, 8)
            sg = AR.f32(2, 8)
            s2 = AR.f32(8, 2)
            wblk = [AR.f32(8, 512) for _ in range(2)]
            mods_sb = AR.f32(6 * D, parts=2)
            bmod_sb = AR.f32(6 * D, parts=2)
            dma("sp", cT, cvec.rearrange("r (p j) -> p r j", j=8), w=["cT"])
            dma("sp", bmod_sb, b_mod[0, :].partition_broadcast(2), w=["bmod"])
            act(sg, cT, AF.Sigmoid, r=["cT"], w=["sg"])
            tt("dve", cT, cT, sg, ALU.mult, r=["cT", "sg"], w=["cT"])
            cp("dve", s2, cT.rearrange("p r j -> p j r"), r=["cT"], w=["s2"])
            wm = w_mod.rearrange("(p j) n -> p j n", j=8)
            for nb in range(2):
                dma("sp", wblk[nb], wm[:, :, nb * 512:(nb + 1) * 512], w=[("wblk", nb)])
            for nb in range(12):
                if nb == 1:
                    late_init()
                wb = wblk[nb % 2]
                if nb >= 2:
                    dma("sp", wb, wm[:, :, nb * 512:(nb + 1) * 512], w=[("wblk", nb % 2)])
                pb = nb % 8
                for j in range(8):
                    mm(bank(pb)[0:2, :], s2[:, j, :], wb[:, j, :], j == 0, j == 7,
                       r=["s2", ("wblk", nb % 2)], w=PS(pb))
                tt("dve", mods_sb[:, nb * 512:(nb + 1) * 512], bank(pb)[0:2, :], bmod_sb[:, nb * 512:(nb + 1) * 512],
                   ALU.add, r=["bmod"], w=PS(pb) + ["mods_sb"])
            dma("sp", mods_scr, mods_sb, r=["mods_sb"], w=["mods_scr"])
            S.barrier()
            AR.release(m0)

        seqs = [dict(kind="s", tok0=0, ntok=DS, nctx=PAST, row=0),
                dict(kind="p", tok0=DS, ntok=NP * SEQ, nctx=0, row=1)]

        def rstd_small(ss, n, scale, parts=128):
            def go(rk, wk):
                act(ss, ss, AF.Ln, r=rk + ["eps"], w=wk, scale=scale, bias=eps_t[0:parts, :])
                act(ss, ss, AF.Exp, r=wk, w=wk, scale=-0.5)
            return go

        def norm_to_bf16(xt, kx, junkA, hg, khg, ssb, kss, hb, khb):
            act(junkA, xt, AF.Square, r=list(kx), w=[khb if junkA is hb else "junkA", kss], accum=ssb)
            rstd_small(ssb, 1, 1.0 / D)([kss], [kss])
            stt("dve", hg, xt, ssb, bc[0], ALU.mult, ALU.mult, r=list(kx) + [kss, BK[0]], w=list(khg))
            tt("dve", hb, hg, bc[1], ALU.add, r=list(khg) + [BK[1]], w=[khb])

        if "A" in phases or "B" in phases:
            mAB = AR.mark()
            TKmax = PAST + DS
            NKTmax = TKmax // 128
            KT_A = AR.bf16(TKmax)
            V_A = AR.bf16(NKTmax, 2 * 65)
            KT_B = AR.bf16(4, TKmax)
            V_B = AR.bf16(NKTmax, 512)
            V_A4 = V_A.rearrange("p k (g e) -> p k g e", e=65)
            memset("dve", V_A4[:, :, :, 64:65], 1.0, w=["VA_ones"])
            cur_row = [None]

            for sq in seqs:
                kind, tok0, ntok, nctx, row = sq["kind"], sq["tok0"], sq["ntok"], sq["nctx"], sq["row"]
                rope_on = kind == "s"
                TK = nctx + ntok
                NKT = TK // 128
                ntile = ntok // 128
                mA = AR.mark()
                if cur_row[0] != row:
                    load_gm(0, row, 1, norm1_g)
                    load_bc(1, row, 0)
                    cur_row[0] = row
                xt = [AR.f32(D) for _ in range(2)]
                junkA = AR.bf16(D)
                hg = AR.f32(D)
                hb = AR.bf16(D)
                hTt = [AR.bf16(8, 128) for _ in range(2)]
                QTt = [AR.bf16(8, 128) for _ in range(2)]
                kvf = AR.f32(1280)
                qf = AR.f32(512)
                sqt = AR.f32(640)
                sm = AR.f32(16)
                ropet = [AR.f32(128) for _ in range(2)]
                rt = AR.f32(16, 64)
                ru = AR.f32(16, 64)
                kb = AR.bf16(640)
                qb = AR.bf16(D)
                if nctx:
                    cbk_sb = AR.bf16(nctx // 128, 512)
                    cak_sb = AR.bf16(nctx // 128, 128)
                    nct = nctx // 128
                    dma("pool", cak_sb, cak.rearrange("(t p) c -> p t c", p=128), w=["cak_sb"])
                    dma("pool", cbk_sb, cbk.rearrange("(t p) c -> p t c", p=128), w=["cbk_sb"])
                    for t in range(nct):
                        dma("pool", V_A4[:, t, :, 0:64],
                            cav[t * 128:(t + 1) * 128, :].rearrange("p (g e) -> p g e", e=64), r=["VA_ones"], w=[("VA", t)])
                    dma("pool", V_B[:, 0:nct, :], cbv.rearrange("(t p) c -> p t c", p=128), w=["VB_ctx"])
                    for t in range(nct):
                        bk = 6 + (t % 2)
                        pT = bankbf(bk)
                        transpose(pT[:, 0:128], cak_sb[:, t, :], r=["cak_sb"], w=PS(bk))
                        for h in range(4):
                            transpose(pT[:, 128 * (h + 1):128 * (h + 2)], cbk_sb[:, t, h * 128:(h + 1) * 128],
                                      r=["cbk_sb"], w=PS(bk))
                        cp("act", KT_A[:, t * 128:(t + 1) * 128], pT[:, 0:128], w=PS(bk) + [("KTA", t)])
                        cp("act", KT_B[:, :, t * 128:(t + 1) * 128],
                           pT[:, 128:640].rearrange("p (h k) -> p h k", k=128), w=PS(bk) + [("KTB", t)])

                def loadx(t):
                    dma("sp", xt[t % 2], x_all[tok0 + t * 128: tok0 + (t + 1) * 128, :], w=[("xt", t % 2)])

                def loadrope(t):
                    if rope_on:
                        dma("sp", ropet[t % 2], rope_d[t * 128:(t + 1) * 128, :], w=[("rope", t % 2)])

                bT, bX, bY, bZ = 0, 1, 2, 3
                bQ = ((4, 5), (6, 7))
                bTK = bTQ = bT
                def normAD(t):
                    s = t % 2
                    norm_to_bf16(xt[s], [("xt", s)], junkA, hg, ["hg"], sm[:, 0:1], "ssA", hb, "hb")

                def tr_h(t):
                    s = t % 2
                    pT = bankbf(bT)
                    for kc in range(8):
                        transpose(pT[:, kc * 128:(kc + 1) * 128], hb[:, kc * 128:(kc + 1) * 128], r=["hb"], w=PS(bT))
                    cp("act", hTt[s], pT.rearrange("p (a b) -> p a b", b=128), w=PS(bT) + [("hTt", s)])
                    dma("sp", hT_scr[:, :, tok0 + t * 128: tok0 + (t + 1) * 128], hTt[s], r=[("hTt", s)], w=["hT_scr"])

                def proj(t, specs):
                    s = t % 2
                    for (bb, c0, n) in specs:
                        for kc in range(8):
                            mm(bank(bb)[:, 0:n], hTt[s][:, kc, :], w_in_sb[:, kc, c0:c0 + n], kc == 0, kc == 7,
                               r=[("hTt", s), "w_in"], w=PS(bb))

                def projQ(t):
                    proj(t, ((bQ[t % 2][0], 0, 512), (bQ[t % 2][1], 768, 512)))

                def projK(t):
                    proj(t, ((bX, 512, 256), (bY, 1280, 512), (bZ, 1792, 512)))

                def rope_blk(s, hf, src, nb, rk, wk_src, dst, dkey, perm=False):
                    rp = ropet[s]
                    C2 = rp[:, 0:64].unsqueeze(1)
                    Sn = rp[:, 64:96].unsqueeze(1)
                    Sp = rp[:, 96:128].unsqueeze(1)
                    kr = ("rope", s)
                    t_ = rt[:, hf * 8:hf * 8 + nb, :]
                    u_ = ru[:, hf * 8:hf * 8 + nb, :]
                    kt_, ku1, ku2 = ("rt", hf), ("ru1", hf), ("ru2", hf)
                    tt("dve", t_, src, C2.to_broadcast([128, nb, 64]), ALU.mult, r=rk + [kr], w=wk_src + [kt_])
                    tt("dve", u_[:, :, 0:32], src[:, :, 32:64], Sn.to_broadcast([128, nb, 32]), ALU.mult,
                       r=rk + [kr], w=wk_src + [ku1])
                    tt("dve", u_[:, :, 32:64], src[:, :, 0:32], Sp.to_broadcast([128, nb, 32]), ALU.mult,
                       r=rk + [kr], w=wk_src + [ku2])
                    if perm:
                        tt("dve", dst, t_.rearrange("p (g r) d -> p g r d", g=2),
                           u_.rearrange("p (g r) d -> p g r d", g=2), ALU.add, r=[kt_, ku1, ku2], w=[dkey])
                    else:
                        tt("dve", dst, t_, u_, ALU.add, r=[kt_, ku1, ku2], w=[dkey])

                qbA = qb[:, 0:512].rearrange("p (r g d) -> p g r d", g=2, d=64)
                qbB = qb[:, 512:1024].rearrange("p (b d) -> p b d", d=64)
                kb3 = kb.rearrange("p (b d) -> p b d", d=64)

                def postQ(t):
                    s = t % 2
                    bQA, bQB = bQ[t % 2]
                    pQA = bank(bQA)
                    pQBv = bank(bQB).rearrange("p (b d) -> p b d", d=64)
                    act(sqt[:, 0:512], pQA, AF.Square, w=PS(bQA) + ["sqq"])
                    redsum(sm[:, 2:10], sqt[:, 0:512].rearrange("p (h d) -> p h d", d=64), r=["sqq"], w=["ssq"])
                    rstd_small(sm[:, 2:10], 8, 1.0 / HD)(["ssq"], ["ssq"])
                    if rope_on:
                        rope_blk(s, 1, pQBv, 8, [], PS(bQB), qbB, "qbB")
                    else:
                        cp("dve", qbB, pQBv, w=PS(bQB) + ["qbB"])
                    qa = qf.rearrange("p (h d) -> p h d", d=64)
                    tt("dve", qa, pQA.rearrange("p (h d) -> p h d", d=64),
                       sm[:, 2:10].unsqueeze(2).to_broadcast([128, 8, 64]), ALU.mult, r=["ssq"], w=PS(bQA) + ["qa"])
                    tt("dve", qa, qa, gq_bc.unsqueeze(1).to_broadcast([128, 8, 64]), ALU.mult, r=["qa", "gq"], w=["qa"])
                    if rope_on:
                        rope_blk(s, 0, qa, 8, ["qa"], [], qbA, "qbA", perm=True)
                    else:
                        cp("dve", qbA, qa.rearrange("p (g r) d -> p g r d", g=2), r=["qa"], w=["qbA"])

                def trQ(t):
                    s = t % 2
                    pTQ = bankbf(bTQ)
                    for j in range(8):
                        transpose(pTQ[:, j * 128:(j + 1) * 128], qb[:, j * 128:(j + 1) * 128], r=["qbA", "qbB"], w=PS(bTQ))
                    cp("act", QTt[s], pTQ.rearrange("p (a b) -> p a b", b=128), w=PS(bTQ) + [("QTt", s)])
                    dma("sp", qT_scr[:, :, tok0 + t * 128: tok0 + (t + 1) * 128], QTt[s], r=[("QTt", s)], w=["qT_scr"])

                def postK(t):
                    s = t % 2
                    kt = nctx // 128 + t
                    pX = bank(bX)
                    pYv = bank(bY).rearrange("p (b d) -> p b d", d=64)
                    act(sqt[:, 512:640], pX[:, 0:128], AF.Square, w=PS(bX) + ["sqk"])
                    redsum(sm[:, 10:12], sqt[:, 512:640].rearrange("p (h d) -> p h d", d=64), r=["sqk"], w=["ssk"])
                    rstd_small(sm[:, 10:12], 2, 1.0 / HD)(["ssk"], ["ssk"])
                    if rope_on:
                        rope_blk(s, 1, pYv, 8, [], PS(bY), kb3[:, 2:10, :], "kbB")
                    akn = kvf[:, 0:128].rearrange("p (g d) -> p g d", d=64)
                    tt("dve", akn, pX[:, 0:128].rearrange("p (g d) -> p g d", d=64),
                       sm[:, 10:12].unsqueeze(2).to_broadcast([128, 2, 64]), ALU.mult, r=["ssk"], w=PS(bX) + ["akn"])
                    tt("dve", akn, akn, gk_bc.unsqueeze(1).to_broadcast([128, 2, 64]), ALU.mult, r=["akn", "gk"], w=["akn"])
                    cp("act", V_A4[:, kt, :, 0:64], pX[:, 128:256].rearrange("p (g d) -> p g d", d=64),
                       r=["VA_ones"], w=PS(bX) + [("VA", kt)])
                    cp("act", V_B[:, kt, :], bank(bZ), w=PS(bZ) + [("VB", kt)])
                    if rope_on:
                        rope_blk(s, 0, akn, 2, ["akn"], [], kb3[:, 0:2, :], "kbA")
                    else:
                        cp("dve", kb[:, 0:128], kvf[:, 0:128], r=["akn"], w=["kbA"])
                        cp("dve", kb[:, 128:640], bank(bY), w=PS(bY) + ["kbB"])
                        cp("dve", kvf[:, 128:256], pX[:, 128:256], w=PS(bX) + ["kvf_v"])
                        cp("act", kvf[:, 256:768], bank(bY), w=PS(bY) + ["kvf_bk"])
                        cp("dve", kvf[:, 768:1280], bank(bZ), w=PS(bZ) + ["kvf_bv"])
                        prow = t * 128
                        dma("sp", nkv[prow:prow + 128, :], kvf, r=["akn", "kvf_v", "kvf_bk", "kvf_bv"])

                def trK(t):
                    kt = nctx // 128 + t
                    pTK = bankbf(bTK)
                    for j in range(5):
                        transpose(pTK[:, j * 128:(j + 1) * 128], kb[:, j * 128:(j + 1) * 128], r=["kbA", "kbB"], w=PS(bTK))
                    cp("act", KT_A[:, kt * 128:(kt + 1) * 128], pTK[:, 0:128], w=PS(bTK) + [("KTA", kt)])
                    cp("act", KT_B[:, :, kt * 128:(kt + 1) * 128],
                       pTK[:, 128:640].rearrange("p (h k) -> p h k", k=128), w=PS(bTK) + [("KTB", kt)])

                if "A" in phases:
                    loadx(0)
                    if ntile > 1:
                        loadx(1)
                    normAD(0)
                    tr_h(0)
                    loadrope(0)
                    if ntile > 1:
                        normAD(1)
                    projQ(0)
                    postQ(0)
                    for t in range(ntile):
                        if t + 2 < ntile:
                            loadx(t + 2)
                        if t + 1 < ntile:
                            loadrope(t + 1)
                            tr_h(t + 1)
                        projK(t)
                        if t + 2 < ntile:
                            normAD(t + 2)
                        if t >= 1:
                            trK(t - 1)
                        trQ(t)
                        postK(t)
                        if t + 1 < ntile:
                            projQ(t + 1)
                            postQ(t + 1)
                    trK(ntile - 1)
                S.barrier()
                AR.release(mA)

                if "B" not in phases:
                    continue
                mB = AR.mark()
                if kind == "s":
                    QC = 512
                    chunks = [dict(t0=q * QC, kts=list(range(NKT))) for q in range(ntok // QC)]
                else:
                    QC = SEQ
                    chunks = [dict(t0=p * SEQ, kts=list(range(p * (SEQ // 128), (p + 1) * (SEQ // 128)))) for p in range(NP)]
                nqc = len(chunks)
                QT = [AR.bf16(8, QC) for _ in range(2)]
                PT = [AR.bf16(2 * QC) for _ in range(3)]
                aoT = AR.bf16(8, QC)
                boT = AR.bf16(4, QC)
                NS = 1 if kind == "s" else 3
                EB = []
                for si in range(NS):
                    EB.append(dict(OsbA=[AR.f32(QC) for _ in range(2)], Oc=[AR.f32(QC) for _ in range(2)],
                                   rc=[AR.f32(QC) for _ in range(2)], dd=AR.f32(QC), d2=AR.bf16(QC), rs=AR.f32(QC)))
                Zacc = AR.f32(QC)
                pair_seq = [0]
                pairs = [dict(mix="A", r=r_) for r_ in range(4)] + [dict(mix="B", h=h) for h in range(4)]
                items = [(qc, pi, kt) for qc in range(nqc) for pi in range(len(pairs)) for kt in chunks[qc]["kts"]]

                def loadq(qc):
                    q0 = chunks[qc]["t0"]
                    dma("sp", QT[qc % 2], qT_scr[:, :, tok0 + q0: tok0 + q0 + QC], r=["qT_scr"], w=[("QT", qc % 2)])

                def emit_S(idx):
                    qc, pi, kt = items[idx]
                    if pi == 0 and kt == chunks[qc]["kts"][0] and qc + 1 < nqc:
                        loadq(qc + 1)
                    m = pairs[pi]
                    slot = idx % 2
                    bks = PS(2 * slot, 2 * slot + 1)
                    Q = QT[qc % 2]
                    for j in range(2):
                        base = j * 64
                        if m["mix"] == "A":
                            qT = Q[base:base + 64, m["r"], :]
                            kT = KT_A[base:base + 64, kt * 128:(kt + 1) * 128]
                        else:
                            qT = Q[base:base + 64, 4 + m["h"], :]
                            kT = KT_B[base:base + 64, m["h"], kt * 128:(kt + 1) * 128]
                        mm(bank(2 * slot + j)[:, 0:QC], kT, qT, True, True, r=[("QT", qc % 2)], w=bks)
                    src = ps[:, slot * 1024:(slot + 1) * 1024].rearrange("p (j q) -> p j q", j=2)[:, :, 0:QC]
                    act(PT[idx % 3].rearrange("p (j q) -> p j q", j=2), src, AF.Exp, w=bks + [("PT", idx % 3)], scale=HD ** -0.5)

                def emit_PV(idx):
                    qc, pi, kt = items[idx]
                    m = pairs[pi]
                    slot = idx % 3
                    first = kt == chunks[qc]["kts"][0]
                    last = kt == chunks[qc]["kts"][-1]
                    if m["mix"] == "A":
                        for j in range(2):
                            mm(bank(4 + j)[0:65, 0:QC], V_A4[:, kt, j, :], PT[slot][:, j * QC:(j + 1) * QC], first, last,
                               r=[("PT", slot)], w=PS(4 + j))
                    else:
                        h = m["h"]
                        for j in range(2):
                            mm(bank(4 + j)[:, 0:QC], V_B[:, kt, h * 128:(h + 1) * 128], PT[slot][:, j * QC:(j + 1) * QC], first, last,
                               r=[("PT", slot)], w=PS(4 + j))
                        mm(bank(6)[:, 0:QC], ones_bf, PT[slot][:, 0:QC], first, last, r=[("PT", slot), "ones_bf"], w=PS(6))
                        if first:
                            cp("dve", Zacc, PT[slot][:, QC:2 * QC], r=[("PT", slot)], w=["Zacc"])
                        else:
                            tt("dve", Zacc, Zacc, PT[slot][:, QC:2 * QC], ALU.add, r=[("PT", slot), "Zacc"], w=["Zacc"])
                    if last:
                        si = pair_seq[0] % NS
                        pair_seq[0] += 1
                        flush_set(si)
                        epilogue_p1(qc, pi, si)
                        defer_epilogue(qc, pi, si)

                pending = []

                def recip_any(pi, out, in_, r=(), w=()):
                    if kind != "s" or pi in (3, 4, 5, 6):
                        act(out, in_, AF.Ln, r=list(r), w=list(w))
                        act(out, out, AF.Exp, r=list(w), w=list(w), scale=-1.0)
                    else:
                        recip(out, in_, r=r, w=w)

                def epilogue_p1(qc, pi, si):
                    m = pairs[pi]
                    B_ = EB[si]
                    if m["mix"] == "A":
                        for j in range(2):
                            cp("dve", B_["OsbA"][j][0:65, :], bank(4 + j)[0:65, 0:QC], w=PS(4 + j) + [("OsbA", si, j)])
                    else:
                        mm(bank(7)[:, 0:QC], ones_f, Zacc, True, True, r=["Zacc", "ones_f"], w=PS(7))
                        for c in range(2):
                            cp("dve", B_["Oc"][c], bank(4 + c)[:, 0:QC], w=PS(4 + c) + [("Oc", si, c)])
                        cp("dve", B_["rs"], bank(6)[:, 0:QC], w=PS(6) + [("rs", si)])
                        recip_any(pi, B_["rc"][1], bank(7)[:, 0:QC], w=PS(7) + [("rc", si, 1)])

                def ep_A2(qc, pi, si, j):
                    m = pairs[pi]
                    B_ = EB[si]
                    hq = j * 4 + m["r"]
                    osb = B_["OsbA"][j]
                    rcj = B_["rc"][j]
                    ko = ("OsbA", si, j)
                    mm(bank(7)[0:64, 0:QC], sel65[0:65, :], osb[0:65, :], True, True, r=[ko, "sel65"], w=PS(7))
                    recip_any(pi, rcj[0:64, :], bank(7)[0:64, 0:QC], w=PS(7) + [("rc", si, j)])
                    tt("dve", aoT[0:64, hq, :], osb[0:64, :], rcj[0:64, :], ALU.mult, r=[ko, ("rc", si, j)], w=["aoT"])

                def ep_B2a(qc, pi, si):
                    B_ = EB[si]
                    Oc_, rc_, dd_, d2_ = B_["Oc"], B_["rc"], B_["dd"], B_["d2"]
                    recip_any(pi, rc_[0], B_["rs"], r=[("rs", si)], w=[("rc", si, 0)])
                    for c in range(2):
                        tt("dve", Oc_[c], Oc_[c], rc_[c], ALU.mult, r=[("rc", si, c), ("Oc", si, c)], w=[("Oc", si, c)])
                    stt("dve", dd_, Oc_[1], neg_lam, Oc_[0], ALU.mult, ALU.add,
                        r=[("Oc", si, 0), ("Oc", si, 1), "neg_lam"], w=[("dd", si)])
                    tt("dve", d2_, dd_, dd_, ALU.mult, r=[("dd", si)], w=[("d2", si)])

                def ep_B2b(qc, pi, si):
                    B_ = EB[si]
                    h = pairs[pi]["h"]
                    rs_ = B_["rs"]
                    mm(bank(7)[:, 0:QC], ones_bf, B_["d2"], True, True, r=[("d2", si), "ones_bf"], w=PS(7))
                    act(rs_, bank(7)[:, 0:QC], AF.Ln, r=["eps"], w=PS(7) + [("rs", si)], scale=1.0 / 128, bias=eps_t)
                    act(rs_, rs_, AF.Exp, r=[("rs", si)], w=[("rs", si)], scale=-0.5)
                    stt("dve", boT[:, h, :], B_["dd"], gs08, rs_, ALU.mult, ALU.mult, r=[("dd", si), ("rs", si), "gs08"], w=["boT"])
                    if pi == len(pairs) - 1:
                        q0 = chunks[qc]["t0"]
                        dma("sp", ao_scr[:, :, tok0 + q0: tok0 + q0 + QC], aoT[0:64, :, :], r=["aoT"], w=["ao_scr"])
                        dma("sp", bo_scr[:, :, tok0 + q0: tok0 + q0 + QC], boT, r=["boT"], w=["bo_scr"])

                def defer_epilogue(qc, pi, si):
                    if pairs[pi]["mix"] == "A":
                        pending.append([3, si, lambda: ep_A2(qc, pi, si, 0)])
                        pending.append([8, si, lambda: ep_A2(qc, pi, si, 1)])
                    else:
                        pending.append([3, si, lambda: ep_B2a(qc, pi, si)])
                        pending.append([11, si, lambda: ep_B2b(qc, pi, si)])

                def flush_set(si):
                    last_i = -1
                    for i_, p_ in enumerate(pending):
                        if p_[1] == si:
                            last_i = i_
                    for _ in range(last_i + 1):
                        pending.pop(0)[2]()

                def run_pending(force=False):
                    while pending and (force or pending[0][0] <= 0):
                        pending.pop(0)[2]()
                    for p_ in pending:
                        p_[0] -= 1

                loadq(0)
                sqi = seqs.index(sq)
                if sqi + 1 < len(seqs) and seqs[sqi + 1]["row"] != cur_row[0] and "A" in phases:
                    nrow = seqs[sqi + 1]["row"]
                    load_gm(0, nrow, 1, norm1_g)
                    load_bc(1, nrow, 0)
                    cur_row[0] = nrow
                if sqi == len(seqs) - 1 and "X" in phases:
                    load_bc(2, 0, 2)
                if kind == "s" and ("X" in phases or "C" in phases):
                    for (dst, src) in ((wg_bf, w_gate), (wba_bf, w_br_a), (wbb_bf, w_br_b), (wo_bf, w_out), (w1_bf, w_fc1), (w2_bf, w_fc2)):
                        for r0 in range(0, src.shape[0], 128):
                            dma("pool", dst[r0:r0 + 128, :], src[r0:r0 + 128, :], w=["wcast"])
                n_it = len(items)
                for idx in range(n_it + 2):
                    if idx < n_it:
                        emit_S(idx)
                    run_pending()
                    if idx >= 2:
                        emit_PV(idx - 2)
                run_pending(force=True)
                S.barrier()
                AR.release(mB)
            S.barrier()
            AR.release(mAB)
        AR.release(m_w_in)

        def groups(gsz):
            out = []
            for t0 in range(0, DS, gsz):
                out.append((t0, gsz, 0))
            tp = NP * SEQ
            t0 = 0
            while t0 < tp:
                n = min(gsz, tp - t0)
                out.append((DS + t0, n, 1))
                t0 += n
            return out

        if "X" in phases:
            mX = AR.mark()
            wg_sb = AR.bf16(8, 2048)
            wba_sb = AR.bf16(4, D)
            wbb_sb = AR.bf16(4, D)
            wo_sb = AR.bf16(8, D)
            wg_r = wg_bf.rearrange("(kc p) n -> p kc n", p=128)
            for kc in range(0, 8, 2):
                dma("act", wg_sb[:, kc:kc + 2, :], wg_r[:, kc:kc + 2, :], w=["wg"])
            dma("pool", wba_sb, wba_bf.rearrange("(h p) n -> p h n", p=128), r=["wg"], w=["wba"])
            dma("pool", wbb_sb, wbb_bf.rearrange("(h p) n -> p h n", p=128), r=["wg"], w=["wbb"])
            dma("pool", wo_sb, wo_bf.rearrange("(kc p) n -> p kc n", p=128), r=["wg"], w=["wo"])
            GS = 512
            hTg = [AR.bf16(8, GS) for _ in range(2)]
            aoG = [AR.bf16(4, GS) for _ in range(2)]
            boG = [AR.bf16(4, GS) for _ in range(2)]
            gT = AR.bf16(16, GS)
            mT = AR.bf16(8, GS)
            t1 = [AR.f32(GS) for _ in range(2)]
            t2 = [AR.f32(GS) for _ in range(2)]
            xg = [AR.f32(D) for _ in range(2)]
            tx = [AR.f32(D) for _ in range(2)]
            grp = groups(GS)
            cur = [0 if ("B" in phases) else None]

            def loadg(gi):
                t0, n, row = grp[gi]
                s = gi % 2
                dma("sp", hTg[s][:, :, 0:n], hT_scr[:, :, t0:t0 + n], w=[("hTg", s)])
                ao_v = ao_scr.rearrange("d (j two) t -> d j two t", two=2)
                dma("sp", aoG[s][0:64, :, 0:n], ao_v[:, :, 0, t0:t0 + n], w=[("aoG", s, 0)])
                dma("sp", aoG[s][64:128, :, 0:n], ao_v[:, :, 1, t0:t0 + n], w=[("aoG", s, 1)])
                dma("sp", boG[s][:, :, 0:n], bo_scr[:, :, t0:t0 + n], w=[("boG", s)])

            loadg(0)
            xi = 0
            for gi, (t0, n, row) in enumerate(grp):
                if gi + 1 < len(grp):
                    loadg(gi + 1)
                s = gi % 2
                if cur[0] != row:
                    load_bc(2, row, 2)
                    cur[0] = row
                for oc in range(16):
                    pb = oc % 2
                    for kc in range(8):
                        mm(bank(pb)[:, 0:n], wg_sb[:, kc, oc * 128:(oc + 1) * 128], hTg[s][:, kc, 0:n], kc == 0, kc == 7,
                           r=["wg", ("hTg", s)], w=PS(pb))
                    act(gT[:, oc, 0:n], bank(pb)[:, 0:n], AF.Sigmoid, r=["bgT"], w=PS(pb) + [("gT", oc)], bias=bgT[:, oc:oc + 1])
                for oc in range(8):
                    pa = 2 + 2 * (oc % 2)
                    pbb = pa + 1
                    for j in range(4):
                        mm(bank(pa)[:, 0:n], wba_sb[:, j, oc * 128:(oc + 1) * 128], aoG[s][:, j, 0:n], j == 0, j == 3,
                           r=["wba", ("aoG", s, 0), ("aoG", s, 1)], w=PS(pa))
                    for h in range(4):
                        mm(bank(pbb)[:, 0:n], wbb_sb[:, h, oc * 128:(oc + 1) * 128], boG[s][:, h, 0:n], h == 0, h == 3,
                           r=["wbb", ("boG", s)], w=PS(pbb))
                    tt("dve", t1[oc % 2][:, 0:n], bank(pa)[:, 0:n], gT[:, oc, 0:n], ALU.mult, r=[("gT", oc)], w=PS(pa) + [("t1", oc % 2)])
                    tt("dve", t2[oc % 2][:, 0:n], bank(pbb)[:, 0:n], gT[:, 8 + oc, 0:n], ALU.mult, r=[("gT", 8 + oc)], w=PS(pbb) + [("t2", oc % 2)])
                    tt("pool", mT[:, oc, 0:n], t1[oc % 2][:, 0:n], t2[oc % 2][:, 0:n], ALU.add,
                       r=[("t1", oc % 2), ("t2", oc % 2)], w=[("mT", oc)])
                for i in range(n // 128):
                    sx = xi % 2
                    xi += 1
                    pbs = (6, 7) if sx == 0 else (0, 1)
                    dma("sp", xg[sx], x_all[t0 + i * 128: t0 + (i + 1) * 128, :], w=[("xg", sx)])
                    for hf in range(2):
                        for kc in range(8):
                            mm(bank(pbs[hf]), mT[:, kc, i * 128:(i + 1) * 128], wo_sb[:, kc, hf * 512:(hf + 1) * 512], kc == 0, kc == 7,
                               r=["wo", ("mT", kc)], w=PS(pbs[hf]))
                    for hf in range(2):
                        tt("dve", tx[sx][:, hf * 512:(hf + 1) * 512], bank(pbs[hf]), bc[2][:, hf * 512:(hf + 1) * 512], ALU.mult,
                           r=[BK[2]], w=PS(pbs[hf]) + [("tx", sx, hf)])
                    tt("pool", tx[sx], tx[sx], xg[sx], ALU.add, r=[("tx", sx, 0), ("tx", sx, 1), ("xg", sx)], w=[("tx", sx, 0), ("tx", sx, 1)])
                    dma("sp", x1_scr[t0 + i * 128: t0 + (i + 1) * 128, :], tx[sx], r=[("tx", sx, 0), ("tx", sx, 1)], w=["x1_scr"])
            S.barrier()
            AR.release(mX)

        if "C" in phases:
            mC = AR.mark()
            w1_sb = AR.bf16(8, 4096)
            w2_sb = AR.bf16(32, D)
            w1_r = w1_bf.rearrange("(kc p) n -> p kc n", p=128)
            w2_r = w2_bf.rearrange("(fc p) n -> p fc n", p=128)
            for kc in range(8):
                dma("act", w1_sb[:, kc, :], w1_r[:, kc, :], w=["w1"])
            for f4 in range(8):
                dma("pool", w2_sb[:, f4 * 4:(f4 + 1) * 4, :], w2_r[:, f4 * 4:(f4 + 1) * 4, :], r=["w1"], w=["w2"])
            GS = 256
            NX1 = 4
            x1t = [AR.f32(D) for _ in range(NX1)]
            hbs = [AR.bf16(D) for _ in range(2)]
            h2T = [AR.bf16(8, GS) for _ in range(2)]
            rr = [AR.f32(2 * GS) for _ in range(2)]
            uT = AR.bf16(32, GS)
            ty = [AR.f32(D) for _ in range(2)]
            smc = AR.f32(8)
            grp = groups(GS)
            cur = [None]
            dma("sp", bc[3], final_norm_g[0, :].partition_broadcast(128), w=[BK[3]])
            xslots = {}
            xi = [0]

            def prep(gi):
                t0, n, row = grp[gi]
                if cur[0] != row:
                    load_gm(0, row, 4, norm2_g)
                    load_bc(1, row, 3)
                    load_bc(2, row, 5)
                    cur[0] = row
                sl = []
                for i in range(n // 128):
                    sx = xi[0] % NX1
                    sy = xi[0] % 2
                    xi[0] += 1
                    sl.append((sx, sy))
                    dma("sp", x1t[sx], x1_scr[t0 + i * 128: t0 + (i + 1) * 128, :], r=["x1_scr"], w=[("x1t", sx)])
                    norm_to_bf16(x1t[sx], [("x1t", sx)], hbs[i], ty[sy], [("ty", sy, 0), ("ty", sy, 1)], smc[:, 2 + i:3 + i], ("ssC", i), hbs[i], ("hb", i))
                xslots[gi] = sl

            def prep_tr(gi):
                t0, n, row = grp[gi]
                hT_ = h2T[gi % 2]
                for i in range(n // 128):
                    pT = bankbf(i)
                    for kc in range(8):
                        transpose(pT[:, kc * 128:(kc + 1) * 128], hbs[i][:, kc * 128:(kc + 1) * 128], r=[("hb", i)], w=PS(i))
                    cp("act", hT_[:, :, i * 128:(i + 1) * 128], pT.rearrange("p (a b) -> p a b", b=128), w=PS(i) + [("h2T", gi % 2)])

            def fc1(gi):
                t0, n, row = grp[gi]
                hT_ = h2T[gi % 2]
                for fp in range(16):
                    pb = 2 + (fp % 2)
                    for k in range(2):
                        fc = fp * 2 + k
                        for kc in range(8):
                            mm(bank(pb)[:, k * GS: k * GS + n], w1_sb[:, kc, fc * 128:(fc + 1) * 128], hT_[:, kc, 0:n], kc == 0, kc == 7,
                               r=["w1", ("h2T", gi % 2)], w=PS(pb))
                    rv = rr[fp % 2].rearrange("p (k t) -> p k t", k=2)
                    act(rv[:, :, 0:n], bank(pb).rearrange("p (k t) -> p k t", k=2)[:, :, 0:n], AF.Relu, w=PS(pb) + [("rr", fp % 2)])
                    tt("dve", uT[:, fp * 2:fp * 2 + 2, 0:n], rv[:, :, 0:n], rv[:, :, 0:n], ALU.mult, r=[("rr", fp % 2)], w=[("uT", fp)])

            def fc2(gi, mid=None):
                t0, n, row = grp[gi]
                for i in range(n // 128):
                    if i == 1 and mid is not None:
                        mid()
                    sx, sy = xslots[gi][i]
                    pbs = (4, 5) if i % 2 == 0 else (6, 7)
                    for hf in range(2):
                        for fc in range(32):
                            mm(bank(pbs[hf]), uT[:, fc, i * 128:(i + 1) * 128], w2_sb[:, fc, hf * 512:(hf + 1) * 512], fc == 0, fc == 31,
                               r=["w2", ("uT", fc // 2)], w=PS(pbs[hf]))
                    for hf in range(2):
                        tt("dve", ty[sy][:, hf * 512:(hf + 1) * 512], bank(pbs[hf]), bc[2][:, hf * 512:(hf + 1) * 512], ALU.mult,
                           r=[BK[2]], w=PS(pbs[hf]) + [("ty", sy, hf)])
                    kty = [("ty", sy, 0), ("ty", sy, 1)]
                    tt("pool", ty[sy], ty[sy], x1t[sx], ALU.add, r=kty + [("x1t", sx)], w=kty)
                    act(x1t[sx], ty[sy], AF.Square, r=kty, w=[("x1t", sx), "ssF"], accum=smc[:, 1:2])
                    rstd_small(smc[:, 1:2], 1, 1.0 / D)(["ssF"], ["ssF"])
                    stt("dve", ty[sy], ty[sy], smc[:, 1:2], bc[3], ALU.mult, ALU.mult, r=kty + ["ssF", BK[3]], w=kty)
                    dma("sp", y_all[t0 + i * 128: t0 + (i + 1) * 128, :], ty[sy], r=kty)

            prep(0)
            prep_tr(0)
            for gi in range(len(grp)):
                has_next = gi + 1 < len(grp)
                nxt_same = has_next and grp[gi + 1][2] == grp[gi][2]
                fc1(gi)
                if nxt_same:
                    prep(gi + 1)
                    if grp[gi][1] >= 256:
                        fc2(gi, mid=lambda g=gi + 1: prep_tr(g))
                    else:
                        fc2(gi)
                        prep_tr(gi + 1)
                else:
                    fc2(gi)
                    if has_next:
                        prep(gi + 1)
                        prep_tr(gi + 1)
            AR.release(mC)

        run = S.emit(eng_sems, dma_sems)
        with nc.Block() as block:
            @block.sync
            def _(e):
                run("sp", e)

            @block.scalar
            def _(e):
                run("act", e)

            @block.tensor
            def _(e):
                run("pe", e)

            @block.vector
            def _(e):
                run("dve", e)

            @block.gpsimd
            def _(e):
                run("pool", e)
    build_program.last_stats = dict(S.stats, arena_peak_words=AR.peak)
    return nc


def make_in_maps(inputs, n_cores, NP, SEQ, DS, PAST):
    f = lambda a: np.ascontiguousarray(np.asarray(a, dtype=np.float32))
    xp = f(inputs["x_prompt"])
    xs = f(inputs["x_sample"])
    shared = {
        "w_mod": f(inputs["w_mod"][0]), "b_mod": f(inputs["b_mod"]), "norm1_g": f(inputs["norm1_g"]),
        "w_in": f(inputs["w_in"][0]), "a_q_norm_g": f(inputs["a_q_norm_g"]), "a_k_norm_g": f(inputs["a_k_norm_g"]),
        "lam_q1": f(inputs["lam_q1"]), "lam_k1": f(inputs["lam_k1"]), "lam_q2": f(inputs["lam_q2"]), "lam_k2": f(inputs["lam_k2"]),
        "b_subln_g": f(inputs["b_subln_g"]), "w_gate": f(inputs["w_gate"][0]), "b_gate": f(inputs["b_gate"]),
        "w_br_a": f(inputs["w_br_a"][0]), "w_br_b": f(inputs["w_br_b"][0]), "w_out": f(inputs["w_out"][0]),
        "norm2_g": f(inputs["norm2_g"]), "w_fc1": f(inputs["w_fc1"][0]), "w_fc2": f(inputs["w_fc2"][0]),
        "final_norm_g": f(inputs["final_norm_g"]).reshape(1, D),
        "ident": np.eye(128, dtype=np.float32),
    }
    maps = []
    for c in range(n_cores):
        m = dict(shared)
        m["x_all"] = np.ascontiguousarray(np.concatenate([xs[c], xp[c * NP:(c + 1) * NP].reshape(NP * SEQ, D)], axis=0))
        m["cak"] = f(inputs["cache_a_k"][c, 0]).reshape(PAST, 128)
        m["cav"] = f(inputs["cache_a_v"][c, 0]).reshape(PAST, 128)
        m["cbk"] = f(inputs["cache_b_k"][c, 0]).reshape(PAST, 512)
        m["cbv"] = f(inputs["cache_b_v"][c, 0]).reshape(PAST, 512)
        m["cvec"] = np.ascontiguousarray(np.stack([f(inputs["c"])[c], f(inputs["c_ctx"])], axis=0))
        maps.append(m)
    return maps


def assemble(results, n_cores, NP, SEQ, DS):
    ys = np.stack([r["y_all"][:DS] for r in results], axis=0)
    yp = np.concatenate([r["y_all"][DS:].reshape(NP, SEQ, D) for r in results], axis=0)
    nkv = np.concatenate([r["nkv"].reshape(NP, SEQ, 1280) for r in results], axis=0)
    B = n_cores * NP
    nak = nkv[:, :, 0:128].reshape(B, 1, SEQ, 2, 64)
    nav = nkv[:, :, 128:256].reshape(B, 1, SEQ, 2, 64)
    nbk = nkv[:, :, 256:768].reshape(B, 1, SEQ, 4, 2, 64)
    nbv = nkv[:, :, 768:1280].reshape(B, 1, SEQ, 4, 128)
    c = lambda a: np.ascontiguousarray(a.astype(np.float32))
    return (c(yp), c(ys), c(nak), c(nav), c(nbk), c(nbv))


def kernel(**inputs):
    NP, SEQ, DS, PAST = 4, 256, 4096, 512
    nc = build_program(NP=NP, SEQ=SEQ, DS=DS, PAST=PAST)
    in_maps = make_in_maps(inputs, N_CORES, NP, SEQ, DS, PAST)
    res = run_bass_kernel_spmd(nc, in_maps, core_ids=list(range(N_CORES)))
    return assemble(res.results, N_CORES, NP, SEQ, DS)
```
